# Optimizing a Trainium2 kernel written in Bass

```python
import jax, jax.numpy as jnp
from jax import lax
import numpy as np

D_MODEL = 1024
BATCH = 2
SEQ = 8192
DEPTH = 1

MLA_HEADS = 8
MLA_Q_LORA = 256
MLA_KV_LORA = 256
MLA_NOPE = 64
MLA_ROPE = 32
MLA_V = 64
ROPE_THETA = 10000.0
DIL_PAIRS = ((128, 1), (512, 4), (2048, 16))
DIL_GROUPS = 3
DIL_HEADS_PER_GROUP = 4
DIL_HEAD_DIM = 128
DIL_HEADS = DIL_GROUPS * DIL_HEADS_PER_GROUP
MEM_LEN = 256
MEM_HEADS = 4
MEM_HEAD_DIM = 128
D_FF = 2816
N_BRANCHES = 3
Q_BLOCK = 128
EPS = 1e-5
ALPHA = (2 * DEPTH) ** 0.25
BETA = (8 * DEPTH) ** -0.25

MLA_OUT = MLA_HEADS * MLA_V
DIL_OUT = DIL_HEADS_PER_GROUP * DIL_HEAD_DIM
DIL_QKV = 3 * DIL_HEADS * DIL_HEAD_DIM
MEM_OUT = MEM_HEADS * MEM_HEAD_DIM
GATE_COLS = N_BRANCHES * D_MODEL
IN_SPLITS = (MLA_Q_LORA, MLA_KV_LORA, MLA_ROPE, DIL_QKV, MEM_OUT, GATE_COLS)
IN_COLS = MLA_Q_LORA + MLA_KV_LORA + MLA_ROPE + DIL_QKV + MEM_OUT + GATE_COLS

kernel_name = "hybrid_gated_mla_dilated_mem_encoder"


def layer_norm(x, g, b):
    xf = x.astype(jnp.float32)
    mu = jnp.mean(xf, -1, keepdims=True)
    var = jnp.mean(jnp.square(xf - mu), -1, keepdims=True)
    y = (xf - mu) * lax.rsqrt(var + EPS) * g.astype(jnp.float32) + b.astype(jnp.float32)
    return y.astype(x.dtype)


def rms_norm(x, g):
    xf = x.astype(jnp.float32)
    y = xf * lax.rsqrt(jnp.mean(jnp.square(xf), -1, keepdims=True) + EPS) * g.astype(jnp.float32)
    return y.astype(x.dtype)


def swiglu(x, w_gate, w_up, w_down):
    return (jax.nn.silu(x @ w_gate) * (x @ w_up)) @ w_down


def rope_tables(seq_len):
    pos = jnp.arange(seq_len, dtype=jnp.float32)
    inv = 1.0 / (ROPE_THETA ** (jnp.arange(0, MLA_ROPE, 2, dtype=jnp.float32) / MLA_ROPE))
    ang = pos[:, None] * inv[None, :]
    return jnp.cos(ang), jnp.sin(ang)


def apply_rope(x, cos, sin):
    cos = cos.astype(x.dtype)
    sin = sin.astype(x.dtype)
    x1, x2 = jnp.split(x, 2, axis=-1)
    return jnp.concatenate([x1 * cos - x2 * sin, x1 * sin + x2 * cos], axis=-1)


def mla_attention(c_q, c_kv, k_rope, q_norm_g, kv_norm_g, w_uq, w_ukv, cos, sin):
    B, S, _ = c_q.shape
    q = (rms_norm(c_q, q_norm_g) @ w_uq).reshape(B, S, MLA_HEADS, MLA_NOPE + MLA_ROPE)
    q_nope, q_pe = q[..., :MLA_NOPE], q[..., MLA_NOPE:]
    q_pe = apply_rope(q_pe, cos[:, None, :], sin[:, None, :])
    kv = (rms_norm(c_kv, kv_norm_g) @ w_ukv).reshape(B, S, MLA_HEADS, MLA_NOPE + MLA_V)
    k_nope, v = kv[..., :MLA_NOPE], kv[..., MLA_NOPE:]
    k_pe = apply_rope(k_rope, cos, sin)
    scale = (MLA_NOPE + MLA_ROPE) ** -0.5
    nb = S // Q_BLOCK
    qn = q_nope.reshape(B, nb, Q_BLOCK, MLA_HEADS, MLA_NOPE).transpose(1, 0, 2, 3, 4)
    qp = q_pe.reshape(B, nb, Q_BLOCK, MLA_HEADS, MLA_ROPE).transpose(1, 0, 2, 3, 4)

    def block(args):
        qn_b, qp_b = args
        s = (jnp.einsum('bqhd,bkhd->bhqk', qn_b, k_nope)
             + jnp.einsum('bqhr,bkr->bhqk', qp_b, k_pe)).astype(jnp.float32) * scale
        p = jax.nn.softmax(s, axis=-1)
        return jnp.einsum('bhqk,bkhd->bqhd', p.astype(v.dtype), v)

    o = lax.map(block, (qn, qp))
    return o.transpose(1, 0, 2, 3, 4).reshape(B, S, MLA_OUT)


def dilated_group(q, k, v, slopes, window, dilation):
    B, S, Hg, Dh = q.shape
    half = window // 2
    n_side = half // dilation
    offs = jnp.arange(-n_side, n_side + 1, dtype=jnp.int32) * dilation
    kp = jnp.pad(k, ((0, 0), (half, half), (0, 0), (0, 0)))
    vp = jnp.pad(v, ((0, 0), (half, half), (0, 0), (0, 0)))
    alibi = -slopes[:, None] * jnp.abs(offs).astype(jnp.float32)[None, :]
    scale = Dh ** -0.5
    nb = S // Q_BLOCK
    qb = q.reshape(B, nb, Q_BLOCK, Hg, Dh).transpose(1, 0, 2, 3, 4)

    def block(args):
        i, q_blk = args
        pos = i * Q_BLOCK + jnp.arange(Q_BLOCK, dtype=jnp.int32)
        kpos = pos[:, None] + offs[None, :]
        valid = (kpos >= 0) & (kpos < S)
        kg = jnp.take(kp, kpos + half, axis=1)
        vg = jnp.take(vp, kpos + half, axis=1)
        s = jnp.einsum('bqhd,bqjhd->bhqj', q_blk, kg).astype(jnp.float32) * scale + alibi[None, :, None, :]
        s = jnp.where(valid[None, None], s, -jnp.inf)
        m = jnp.max(s, axis=-1, keepdims=True)
        e = jnp.exp(s - m)
        den = jnp.sum(e, axis=-1, keepdims=True)
        o = jnp.einsum('bhqj,bqjhd->bqhd', (e / den).astype(v.dtype), vg)
        lse = (m + jnp.log(den))[..., 0].transpose(0, 2, 1)
        return o, lse

    o, lse = lax.map(block, (jnp.arange(nb, dtype=jnp.int32), qb))
    o = o.transpose(1, 0, 2, 3, 4).reshape(B, S, Hg, Dh)
    lse = lse.transpose(1, 0, 2, 3).reshape(B, S, Hg)
    return o, lse


def dilated_attention(dil_cols):
    B, S, _ = dil_cols.shape
    qkv = dil_cols.reshape(B, S, 3, DIL_GROUPS, DIL_HEADS_PER_GROUP, DIL_HEAD_DIM)
    slopes = (2.0 ** (-8.0 * jnp.arange(1, DIL_HEADS + 1, dtype=jnp.float32) / DIL_HEADS)).reshape(
        DIL_GROUPS, DIL_HEADS_PER_GROUP)
    outs, lses = [], []
    for g, (window, dilation) in enumerate(DIL_PAIRS):
        o, lse = dilated_group(qkv[:, :, 0, g], qkv[:, :, 1, g], qkv[:, :, 2, g], slopes[g], window, dilation)
        outs.append(o)
        lses.append(lse)
    w = jax.nn.softmax(jnp.stack(lses, 0), axis=0)
    o = jnp.sum(w[..., None].astype(dil_cols.dtype) * jnp.stack(outs, 0), axis=0)
    return o.reshape(B, S, DIL_OUT)


def memory_attention(q_cols, mem, w_mem_kv):
    B, S, _ = q_cols.shape
    M = mem.shape[1]
    q = q_cols.reshape(B, S, MEM_HEADS, MEM_HEAD_DIM)
    kv = (mem @ w_mem_kv).reshape(B, M, 2, MEM_HEADS, MEM_HEAD_DIM)
    k, v = kv[:, :, 0], kv[:, :, 1]
    s = jnp.einsum('bshd,bmhd->bhsm', q, k).astype(jnp.float32) * (MEM_HEAD_DIM ** -0.5)
    p = jax.nn.softmax(s, axis=-1)
    return jnp.einsum('bhsm,bmhd->bshd', p.astype(v.dtype), v).reshape(B, S, MEM_OUT)


def token_mixing(u, mem, w_in, q_norm_g, kv_norm_g, w_uq, w_ukv, w_mem_kv,
                 w_br_mla, w_br_dil, w_br_mem, w_o, cos, sin):
    B, S, D = u.shape
    proj = u @ w_in
    c_q, c_kv, k_rope, dil_cols, memq_cols, gate_cols = jnp.split(
        proj, [int(i) for i in np.cumsum(IN_SPLITS)[:-1]], axis=-1)
    y_a = mla_attention(c_q, c_kv, k_rope, q_norm_g, kv_norm_g, w_uq, w_ukv, cos, sin) @ w_br_mla
    y_b = dilated_attention(dil_cols) @ w_br_dil
    y_c = memory_attention(memq_cols, mem, w_mem_kv) @ w_br_mem
    gates = jax.nn.sigmoid(gate_cols.astype(jnp.float32)).astype(u.dtype).reshape(B, S, N_BRANCHES, D)
    merged = gates[:, :, 0] * y_a + gates[:, :, 1] * y_b + gates[:, :, 2] * y_c
    return merged @ w_o


def setup_inputs(seed: int = 0) -> dict:
    key = jax.random.key(seed)
    ks = jax.random.split(key, 24)
    f32 = jnp.float32

    def nrm(k, shape, fan_in, gain=1.0):
        return jax.random.normal(k, shape, f32) * (fan_in ** -0.5) * gain

    def gain(k, shape):
        return 1.0 + 0.01 * jax.random.normal(k, shape, f32)

    def bias(k, shape):
        return 0.01 * jax.random.normal(k, shape, f32)

    L, D = DEPTH, D_MODEL
    return {
        "x": jax.random.normal(ks[0], (BATCH, SEQ, D), f32),
        "mem": jax.random.normal(ks[1], (BATCH, MEM_LEN, D), f32),
        "w_in": nrm(ks[2], (L, D, IN_COLS), D),
        "mla_q_norm": gain(ks[3], (L, MLA_Q_LORA)),
        "mla_kv_norm": gain(ks[4], (L, MLA_KV_LORA)),
        "w_uq": nrm(ks[5], (L, MLA_Q_LORA, MLA_HEADS * (MLA_NOPE + MLA_ROPE)), MLA_Q_LORA),
        "w_ukv": nrm(ks[6], (L, MLA_KV_LORA, MLA_HEADS * (MLA_NOPE + MLA_V)), MLA_KV_LORA),
        "w_mem_kv": nrm(ks[7], (L, D, 2 * MEM_OUT), D),
        "w_br_mla": nrm(ks[8], (L, MLA_OUT, D), MLA_OUT),
        "w_br_dil": nrm(ks[9], (L, DIL_OUT, D), DIL_OUT),
        "w_br_mem": nrm(ks[10], (L, MEM_OUT, D), MEM_OUT),
        "w_o": nrm(ks[11], (L, D, D), D, BETA),
        "ffn1_w_gate": nrm(ks[12], (L, D, D_FF), D),
        "ffn1_w_up": nrm(ks[13], (L, D, D_FF), D),
        "ffn1_w_down": nrm(ks[14], (L, D_FF, D), D_FF, BETA),
        "ffn2_w_gate": nrm(ks[15], (L, D, D_FF), D),
        "ffn2_w_up": nrm(ks[16], (L, D, D_FF), D),
        "ffn2_w_down": nrm(ks[17], (L, D_FF, D), D_FF, BETA),
        "ln1_g": gain(ks[18], (L, D)),
        "ln1_b": bias(ks[19], (L, D)),
        "ln2_g": gain(ks[20], (L, D)),
        "ln2_b": bias(ks[21], (L, D)),
        "ln3_g": gain(ks[22], (L, D)),
        "ln3_b": bias(ks[23], (L, D)),
    }


def reference(x, mem, w_in, mla_q_norm, mla_kv_norm, w_uq, w_ukv, w_mem_kv,
              w_br_mla, w_br_dil, w_br_mem, w_o,
              ffn1_w_gate, ffn1_w_up, ffn1_w_down, ffn2_w_gate, ffn2_w_up, ffn2_w_down,
              ln1_g, ln1_b, ln2_g, ln2_b, ln3_g, ln3_b):
    cos, sin = rope_tables(x.shape[1])
    h = x
    for l in range(DEPTH):
        h = layer_norm(ALPHA * h + 0.5 * swiglu(h, ffn1_w_gate[l], ffn1_w_up[l], ffn1_w_down[l]),
                       ln1_g[l], ln1_b[l])
        mix = token_mixing(h, mem, w_in[l], mla_q_norm[l], mla_kv_norm[l], w_uq[l], w_ukv[l], w_mem_kv[l],
                           w_br_mla[l], w_br_dil[l], w_br_mem[l], w_o[l], cos, sin)
        h = layer_norm(ALPHA * h + mix, ln2_g[l], ln2_b[l])
        h = layer_norm(ALPHA * h + 0.5 * swiglu(h, ffn2_w_gate[l], ffn2_w_up[l], ffn2_w_down[l]),
                       ln3_g[l], ln3_b[l])
    return h
```

```python
import numpy as np
import ml_dtypes
from contextlib import ExitStack
import concourse.bass as bass
import concourse.mybir as mybir
from concourse.bass_utils import run_bass_kernel_spmd

F32 = mybir.dt.float32
BF16 = mybir.dt.bfloat16
AF = mybir.ActivationFunctionType
ALU = mybir.AluOpType
AX = mybir.AxisListType

D = 1024
S = 8192
NOWN = 2048
FF = 2816
NFC = 22
EPS = 1e-5
ALPHA = 2.0 ** 0.25
C_CQ, C_CKV, C_KR, C_DIL, C_MEMQ, C_GATE = 0, 256, 512, 544, 5152, 5664
MLA_SCALE = 96.0 ** -0.5
DH_SCALE = 128.0 ** -0.5
DIL = ((128, 1), (512, 4), (2048, 16))


class Buf:
    __slots__ = ("name", "w", "r")

    def __init__(self, name=""):
        self.name = name
        self.w = None
        self.r = []


class Ctx:
    NDMASEM = 48

    def __init__(self, nc):
        self.nc = nc
        self.eng = {"pe": nc.tensor, "act": nc.scalar, "dve": nc.vector,
                    "pool": nc.gpsimd, "sp": nc.sync}
        self.sem = {}
        self.cnt = {}
        for e in ("pe", "act", "dve", "pool"):
            self.sem[e] = nc.alloc_semaphore("c_" + e)
            self.cnt[e] = 0
        self.dsem = [nc.alloc_semaphore("d%d" % i) for i in range(self.NDMASEM)]
        self.dcnt = [0] * self.NDMASEM
        self.dpool = {"sp": list(range(0, 32)), "pool": list(range(32, self.NDMASEM))}
        self.dnext = {"sp": 0, "pool": 0}
        self.seen = {e: {} for e in self.eng}

    def _wait(self, eng, tok):
        key, val = tok
        if key == "pe" and eng == "pe":
            return
        s = self.seen[eng]
        if s.get(key, 0) >= val:
            return
        s[key] = val
        sem = self.sem[key] if isinstance(key, str) else self.dsem[key]
        self.eng[eng].wait_ge(sem, val)

    def _deps(self, eng, reads, writes):
        for b in reads:
            if b.w is not None:
                self._wait(eng, b.w)
        for b in writes:
            if b.w is not None:
                self._wait(eng, b.w)
            for t in b.r:
                self._wait(eng, t)

    def _commit(self, tok, reads, writes):
        for b in reads:
            b.r.append(tok)
            if len(b.r) > 48:
                d = {}
                for k, v in b.r:
                    if d.get(k, 0) < v:
                        d[k] = v
                b.r = list(d.items())
        for b in writes:
            b.w = tok
            b.r = []

    def op(self, eng, fn, reads=(), writes=()):
        self._deps(eng, reads, writes)
        ins = fn(self.eng[eng])
        self.cnt[eng] += 1
        ins.then_inc(self.sem[eng], 1)
        self._commit((eng, self.cnt[eng]), reads, writes)
        return ins

    def chain(self, eng, fns, reads=(), writes=()):
        self._deps(eng, reads, writes)
        ins = None
        for fn in fns:
            ins = fn(self.eng[eng])
        self.cnt[eng] += 1
        ins.then_inc(self.sem[eng], 1)
        self._commit((eng, self.cnt[eng]), reads, writes)

    def dma(self, q, fns, reads=(), writes=()):
        if not isinstance(fns, (list, tuple)):
            fns = [fns]
        self._deps(q, reads, writes)
        pl = self.dpool[q]
        i = pl[self.dnext[q]]
        self.dnext[q] = (self.dnext[q] + 1) % len(pl)
        if self.dcnt[i] > 0:
            self._wait(q, (i, self.dcnt[i]))
        for fn in fns:
            ins = fn(self.eng[q])
            ins.then_inc(self.dsem[i], 16)
            self.dcnt[i] += 16
        self._commit((i, self.dcnt[i]), reads, writes)

    def all_tokens(self):
        toks = [(e, self.cnt[e]) for e in self.cnt if self.cnt[e] > 0]
        toks += [(i, n) for i, n in enumerate(self.dcnt) if n > 0]
        return toks

    def barrier(self):
        toks = self.all_tokens()
        for e in self.eng:
            for t in toks:
                self._wait(e, t)

    def finish(self, eng="sp"):
        for t in self.all_tokens():
            self._wait(eng, t)


def mm(c, out_ap, pairs, reads, out_buf):
    n = len(pairs)
    fns = []
    for i, (l, r) in enumerate(pairs):
        fns.append(lambda e, l=l, r=r, i=i: e.matmul(out_ap, lhsT=l, rhs=r, start=(i == 0), stop=(i == n - 1)))
    c.chain("pe", fns, reads=reads, writes=[out_buf])


class K:
    pass


def load_w(c, dst_tile, dst_buf, src, nk, q="pool"):
    for k in range(nk):
        c.dma(q, lambda e, k=k: e.dma_start(out=dst_tile[:, k, :], in_=src[k * 128:(k + 1) * 128, :]),
              writes=[dst_buf])


def ffn_phase(g, x_d, nblk, wg_d, wu_d, wd_d, lng_d, lnb_d, first, pfx):
    nc, c = g.nc, g.c
    with ExitStack() as es:
        def sb(name, shape, dt):
            return es.enter_context(nc.sbuf_tensor(pfx + name, shape, dt))

        def ps(name, shape, dt):
            return es.enter_context(nc.psum_tensor(pfx + name, shape, dt))

        wg = sb("wg", [128, 8, FF], BF16); b_wg = Buf()
        wu = sb("wu", [128, 8, FF], BF16); b_wu = Buf()
        wd = sb("wd", [128, NFC, D], BF16); b_wd = Buf()
        lng = sb("lng", [128, D], F32); lnb = sb("lnb", [128, D], F32); b_ln = Buf()
        TB = 256
        NT = TB // 128
        xb = [sb("xb%d" % i, [128, NT, D], BF16) for i in range(2)]; b_xb = [Buf(), Buf()]
        xr = [sb("xr%d" % i, [128, D], F32) for i in range(2)]; b_xr = [Buf(), Buf()]
        xT = sb("xT", [128, 8, TB], BF16); b_xT = Buf()
        hT = sb("hT", [128, NFC, TB], BF16); b_hT = [Buf() for _ in range(NFC)]
        sg = [sb("sg%d" % i, [128, TB], F32) for i in range(2)]; b_sg = [Buf(), Buf()]
        z = [sb("z%d" % i, [128, D], F32) for i in range(1)]; b_z = [Buf()]
        yn = [sb("yn%d" % i, [128, D], F32) for i in range(2)]; b_yn = [Buf(), Buf()]
        st = sb("st", [128, 2, 6], F32); mv = sb("mv", [128, 2], F32)
        rs = sb("rs", [128, 4], F32); b_st = Buf()
        pG = [ps("pG%d" % i, [128, TB], F32) for i in range(2)]; b_pG = [Buf(), Buf()]
        pU = [ps("pU%d" % i, [128, TB], F32) for i in range(2)]; b_pU = [Buf(), Buf()]
        pY = [ps("pY%d" % i, [128, 512], F32) for i in range(2)]; b_pY = [Buf(), Buf()]
        pT = ps("pT", [128, 8, 128], BF16); b_pT = Buf()
        if first:
            pL = ps("pL", [128, 512], F32); b_pL = Buf()
            wlat = sb("wlat", [128, 8, 288], BF16); b_wlat = Buf()
            kvg = sb("kvg", [128, 256], F32)
            cosk = [sb("cosk%d" % i, [128, NT, 16], F32) for i in range(2)]
            sink = [sb("sink%d" % i, [128, NT, 16], F32) for i in range(2)]
            b_cs = [Buf(), Buf()]; b_tab = Buf()
            hb = sb("hb", [128, D], BF16); b_hb = Buf()
            h1T = sb("h1T", [128, 8, TB], BF16); b_h1T = Buf()
            sq = sb("sq", [128, 256], F32); b_sq = Buf()
            kn = sb("kn", [128, 256], BF16); kpe = sb("kpe", [128, 32], BF16); b_kn = Buf()
            rt = sb("rt", [128, 4, 16], F32)
            knTb = sb("knTb", [128, 2, TB], BF16); kpeTb = sb("kpeTb", [32, TB], BF16); b_kT = Buf()

        def load_x(b):
            i = b % 2
            c.dma("pool", lambda e: e.dma_start(
                out=xb[i][:], in_=x_d[b * TB:(b + 1) * TB, :].rearrange("(t p) n -> p t n", p=128)),
                writes=[b_xb[i]])
            if first:
                c.dma("sp", [lambda e: e.dma_start(out=cosk[i][:], in_=g.d["cosk"][:, b * NT:(b + 1) * NT, :]),
                             lambda e: e.dma_start(out=sink[i][:], in_=g.d["sink"][:, b * NT:(b + 1) * NT, :])],
                      writes=[b_cs[i]])

        def load_xr(gt):
            i = gt % 2
            c.dma("sp", lambda e: e.dma_start(out=xr[i][:], in_=x_d[gt * 128:(gt + 1) * 128, :]),
                  writes=[b_xr[i]])
        c.dma("sp", [lambda e: e.dma_start(out=lng[:], in_=lng_d), lambda e: e.dma_start(out=lnb[:], in_=lnb_d)],
              writes=[b_ln])
        if first:
            c.dma("sp", [lambda e: e.dma_start(out=kvg[:], in_=g.d["kvg"])], writes=[b_tab])
        load_w(c, wg, b_wg, wg_d, 8)
        load_w(c, wu, b_wu, wu_d, 8)
        if first:
            c.dma("pool", lambda e: e.dma_start(
                out=wlat[:], in_=g.d["w_in"][:, C_CKV:C_CKV + 288].rearrange("(k p) n -> p k n", p=128)),
                writes=[b_wlat])
        load_x(0)
        load_w(c, wd, b_wd, wd_d, NFC)
        load_xr(0)

        for b in range(nblk):
            i = b % 2
            if b + 1 < nblk:
                load_x(b + 1)
            for t in range(NT):
                c.chain("pe", [lambda e, k=k: e.transpose(out=pT[:, k, :], in_=xb[i][:, t, k * 128:(k + 1) * 128],
                                                          identity=g.ident[:]) for k in range(8)],
                        reads=[b_xb[i], g.b_const], writes=[b_pT])
                c.op("dve", lambda e: e.tensor_copy(out=xT[:, :, t * 128:(t + 1) * 128], in_=pT[:]),
                     reads=[b_pT], writes=[b_xT])
            for cc in range(NFC):
                j = cc % 2
                cs = slice(cc * 128, (cc + 1) * 128)
                mm(c, pG[j][:], [(wg[:, k, cs], xT[:, k, :]) for k in range(8)], [b_wg, b_xT], b_pG[j])
                mm(c, pU[j][:], [(wu[:, k, cs], xT[:, k, :]) for k in range(8)], [b_wu, b_xT], b_pU[j])
                c.op("act", lambda e: e.activation(out=sg[j][:], in_=pG[j][:], func=AF.Silu),
                     reads=[b_pG[j]], writes=[b_sg[j]])
                c.op("dve", lambda e: e.tensor_tensor(out=hT[:, cc, :], in0=sg[j][:], in1=pU[j][:], op=ALU.mult),
                     reads=[b_sg[j], b_pU[j]], writes=[b_hT[cc]])
            for t in range(NT):
                zi = 0
                gt = b * NT + t
                ri = gt % 2
                if gt + 1 < nblk * NT:
                    load_xr(gt + 1)
                ts = slice(t * 128, (t + 1) * 128)
                for h in range(2):
                    mm(c, pY[h][:], [(hT[:, cc, ts], wd[:, cc, h * 512:(h + 1) * 512]) for cc in range(NFC)],
                       b_hT + [b_wd], b_pY[h])
                c.op("act", lambda e: e.activation(out=z[zi][:], in_=xr[ri][:], func=AF.Copy, scale=ALPHA),
                     reads=[b_xr[ri]], writes=[b_z[zi]])
                for h in range(2):
                    hs = slice(h * 512, (h + 1) * 512)
                    c.op("dve", lambda e: e.scalar_tensor_tensor(out=z[zi][:, hs], in0=pY[h][:], scalar=0.5,
                                                                  in1=z[zi][:, hs], op0=ALU.mult, op1=ALU.add),
                         reads=[b_pY[h], b_z[zi]], writes=[b_z[zi]])
                zi = ri
                layer_norm_tile(g, z[0], b_z[0], yn[zi], b_yn[zi], st, mv, rs, b_st, lng, lnb, b_ln)
                if first:
                    if gt < 16:
                        c.dma("sp", lambda e: e.dma_start(out=g.d["h1own"][gt * 128:(gt + 1) * 128, :], in_=yn[zi][:]),
                              reads=[b_yn[zi]], writes=[g.b_h1own])
                    c.op("act", lambda e: e.activation(out=hb[:], in_=yn[zi][:], func=AF.Copy),
                         reads=[b_yn[zi]], writes=[b_hb])
                    c.chain("pe", [lambda e, k=k: e.transpose(out=pT[:, k, :], in_=hb[:, k * 128:(k + 1) * 128],
                                                              identity=g.ident[:]) for k in range(8)],
                            reads=[b_hb, g.b_const], writes=[b_pT])
                    c.op("dve", lambda e: e.tensor_copy(out=h1T[:, :, ts], in_=pT[:]),
                         reads=[b_pT], writes=[b_h1T])
                    mm(c, pL[:, 0:288], [(h1T[:, k, ts], wlat[:, k, :]) for k in range(8)], [b_h1T, b_wlat], b_pL)
                    rms_tile(g, pL[:, 0:256], b_pL, sq, b_sq, rs, b_st, kvg, b_tab, kn[:], b_kn)
                    rope_tile(g, pL[:, 256:272], pL[:, 272:288], b_pL, cosk[i][:, t, :], sink[i][:, t, :], b_cs[i],
                              rt, b_sq, kpe[:, 0:16], kpe[:, 16:32], b_kn)
                    c.chain("pe", [lambda e, k=k: e.transpose(out=pT[:, k, :], in_=kn[:, k * 128:(k + 1) * 128],
                                                              identity=g.ident[:]) for k in range(2)] +
                            [lambda e: e.transpose(out=pT[0:32, 2, :], in_=kpe[:], identity=g.ident[:])],
                            reads=[b_kn, g.b_const], writes=[b_pT])
                    c.op("dve", lambda e: e.tensor_copy(out=knTb[:, :, ts], in_=pT[:, 0:2, :]),
                         reads=[b_pT], writes=[b_kT])
                    c.op("dve", lambda e: e.tensor_copy(out=kpeTb[:, ts], in_=pT[0:32, 2, :]),
                         reads=[b_pT], writes=[b_kT])
                else:
                    c.dma("sp", lambda e: e.dma_start(out=g.d["out"][gt * 128:(gt + 1) * 128, :], in_=yn[zi][:]),
                          reads=[b_yn[zi]], writes=[g.b_out])
            if first:
                bs = slice(b * TB, (b + 1) * TB)
                c.dma("sp", [lambda e: e.dma_start(out=g.d["knT"].rearrange("(k p) n -> p k n", p=128)[:, :, bs],
                                                   in_=knTb[:]),
                             lambda e: e.dma_start(out=g.d["kpeT"][:, bs], in_=kpeTb[:])],
                      reads=[b_kT], writes=[g.b_knT])
                ext = None
                tok0 = b * TB
                if tok0 < 2048:
                    ext = 1024 + tok0
                elif 3072 <= tok0 < 4096:
                    ext = tok0 - 3072
                elif 4096 <= tok0 < 5120:
                    ext = 3072 + (tok0 - 4096)
                if ext is not None:
                    c.dma("sp", lambda e: e.dma_start(
                        out=g.d["h1T"].rearrange("(k p) n -> p k n", p=128)[:, :, ext:ext + TB], in_=h1T[:]),
                        reads=[b_h1T], writes=[g.b_h1T])
        c.barrier()


def layer_norm_tile(g, zt, b_z, yt, b_y, st, mv, rs, b_st, lng, lnb, b_ln):
    c = g.c
    c.op("dve", lambda e: e.bn_stats(out=st[:, 0, :], in_=zt[:, 0:512]), reads=[b_z], writes=[b_st])
    c.op("dve", lambda e: e.bn_stats(out=st[:, 1, :], in_=zt[:, 512:1024]), reads=[b_z], writes=[b_st])
    c.op("dve", lambda e: e.bn_aggr(out=mv[:], in_=st[:]), reads=[b_st], writes=[b_st])
    c.op("act", lambda e: e.activation(out=rs[:, 0:1], in_=mv[:, 1:2], func=AF.Sqrt, bias=g.epsc[:, 0:1], scale=1.0),
         reads=[b_st, g.b_const], writes=[b_st])
    c.op("dve", lambda e: e.reciprocal(out=rs[:, 0:1], in_=rs[:, 0:1]), reads=[b_st], writes=[b_st])
    c.op("dve", lambda e: e.scalar_tensor_tensor(out=rs[:, 1:2], in0=mv[:, 0:1], scalar=-1.0, in1=rs[:, 0:1],
                                                 op0=ALU.mult, op1=ALU.mult), reads=[b_st], writes=[b_st])
    c.op("act", lambda e: e.activation(out=yt[:], in_=zt[:], func=AF.Identity, bias=rs[:, 1:2], scale=rs[:, 0:1]),
         reads=[b_z, b_st], writes=[b_y])
    c.op("pool", lambda e: e.tensor_tensor(out=yt[:], in0=yt[:], in1=lng[:], op=ALU.mult),
         reads=[b_y, b_ln], writes=[b_y])
    c.op("pool", lambda e: e.tensor_tensor(out=yt[:], in0=yt[:], in1=lnb[:], op=ALU.add),
         reads=[b_y, b_ln], writes=[b_y])


def rms_tile(g, src, b_src, sq, b_sq, rs, b_st, gain, b_gain, dst, b_dst):
    c = g.c
    c.op("act", lambda e: e.activation(out=sq[:], in_=src, func=AF.Square), reads=[b_src], writes=[b_sq])
    c.op("dve", lambda e: e.reduce_sum(out=rs[:, 2:3], in_=sq[:], axis=AX.X), reads=[b_sq], writes=[b_st])
    c.op("act", lambda e: e.activation(out=rs[:, 3:4], in_=rs[:, 2:3], func=AF.Sqrt, bias=g.epsc[:, 0:1],
                                       scale=1.0 / 256), reads=[b_st, g.b_const], writes=[b_st])
    c.op("dve", lambda e: e.reciprocal(out=rs[:, 3:4], in_=rs[:, 3:4]), reads=[b_st], writes=[b_st])
    c.op("dve", lambda e: e.scalar_tensor_tensor(out=dst, in0=src, scalar=rs[:, 3:4], in1=gain[:],
                                                 op0=ALU.mult, op1=ALU.mult),
         reads=[b_src, b_st, b_gain], writes=[b_dst])


def rope_tile(g, x1, x2, b_x, cos, sin, b_tab, rt, b_rt, o1, o2, b_o):
    c = g.c
    sh = list(x1.shape)
    if len(sh) == 2:
        t = [rt[:, i, :] for i in range(4)]
    else:
        t = [rt[:, i] for i in range(4)]
    c.op("dve", lambda e: e.tensor_tensor(out=t[0], in0=x1, in1=cos, op=ALU.mult), reads=[b_x, b_tab], writes=[b_rt])
    c.op("dve", lambda e: e.tensor_tensor(out=t[1], in0=x2, in1=sin, op=ALU.mult), reads=[b_x, b_tab], writes=[b_rt])
    c.op("dve", lambda e: e.tensor_tensor(out=t[2], in0=x1, in1=sin, op=ALU.mult), reads=[b_x, b_tab], writes=[b_rt])
    c.op("dve", lambda e: e.tensor_tensor(out=t[3], in0=x2, in1=cos, op=ALU.mult), reads=[b_x, b_tab], writes=[b_rt])
    c.op("dve", lambda e: e.tensor_tensor(out=o1, in0=t[0], in1=t[1], op=ALU.subtract), reads=[b_rt], writes=[b_o])
    c.op("dve", lambda e: e.tensor_tensor(out=o2, in0=t[2], in1=t[3], op=ALU.add), reads=[b_rt], writes=[b_o])


def mla_phase(g):
    nc, c = g.nc, g.c
    with ExitStack() as es:
        def sb(name, shape, dt):
            return es.enter_context(nc.sbuf_tensor("m_" + name, shape, dt))

        def ps(name, shape, dt):
            return es.enter_context(nc.psum_tensor("m_" + name, shape, dt))

        h1o = sb("h1o", [128, 8, NOWN], BF16); b_h1o = Buf()
        knT = sb("knT", [128, 2, S], BF16); b_knT = Buf()
        kcat = sb("kcat", [96, S], BF16); b_kpe = Buf(); b_kn = [Buf() for _ in range(16)]
        qcT = sb("qcT", [96, 8, NOWN], BF16); b_qcT = Buf()
        qnT = sb("qnT", [128, 2, NOWN], BF16); b_qnT = Buf()
        wq = sb("wq", [128, 8, 256], BF16); wuq = sb("wuq", [128, 2, 768], BF16)
        wukv = sb("wukv", [128, 2, 1024], BF16); b_w = Buf()
        qg = sb("qg", [128, 256], F32); cosq = sb("cosq", [128, 16, 16], F32); sinq = sb("sinq", [128, 16, 16], F32)
        b_tab = Buf()
        vaug = sb("vaug", [128, 64, 65], BF16); b_vaug = Buf()
        pt = [sb("pt%d" % i, [128, 512], BF16) for i in range(3)]; b_pt = [Buf() for _ in range(3)]
        sq = sb("sq", [128, 256], F32); b_sq = Buf()
        rs = sb("rs", [128, 4], F32); b_st = Buf()
        qn = sb("qn", [128, 256], BF16); b_qn = Buf()
        qf = sb("qf", [128, 8, 96], F32); b_qf = Buf()
        qc = sb("qc", [128, 8, 96], BF16); b_qc = Buf()
        rt = sb("rt", [128, 4, 8, 16], F32); b_rt = Buf()
        rrow = sb("rrow", [65, 512], F32); b_rrow = Buf()
        bc = sb("bc", [64, 512], F32); b_bc = Buf()
        ob = [sb("ob%d" % i, [64, 512], BF16) for i in range(2)]; b_ob = [Buf(), Buf()]
        pA = ps("pA", [128, 512], F32); b_pA = Buf()
        pB = ps("pB", [128, 512], F32); b_pB = Buf()
        pV = ps("pV", [128, 8, 64], F32); b_pV = Buf()
        pS = [ps("pS%d" % i, [128, 512], F32) for i in range(3)]; b_pS = [Buf() for _ in range(3)]
        pO = ps("pO", [128, 512], F32); b_pO = Buf()
        pT = ps("pT", [128, 8, 128], BF16); b_pT = Buf()

        h1v = g.d["h1T"].rearrange("(k p) n -> p k n", p=128)
        for k in range(8):
            c.dma("sp", lambda e: e.dma_start(out=h1o[:, k, :], in_=h1v[:, k, 1024:3072]), writes=[b_h1o])
        c.dma("sp", [lambda e: e.dma_start(out=qg[:], in_=g.d["qg"]),
                     lambda e: e.dma_start(out=cosq[:], in_=g.d["cosk"][:, 0:16, :]),
                     lambda e: e.dma_start(out=sinq[:], in_=g.d["sink"][:, 0:16, :])], writes=[b_tab])
        c.dma("pool", [lambda e: e.dma_start(out=wq[:], in_=g.d["w_in"][:, C_CQ:C_CQ + 256].rearrange(
            "(k p) n -> p k n", p=128)),
            lambda e: e.dma_start(out=wuq[:], in_=g.d["w_uq"].rearrange("(k p) n -> p k n", p=128)),
            lambda e: e.dma_start(out=wukv[:], in_=g.d["w_ukv"].rearrange("(k p) n -> p k n", p=128))],
            writes=[b_w])
        knv = g.d["knT"].rearrange("(k p) n -> p k n", p=128)
        for k in range(2):
            c.dma("sp", lambda e: e.dma_start(out=knT[:, k, :], in_=knv[:, k, :]), writes=[b_knT])
        c.dma("sp", lambda e: e.dma_start(out=kcat[64:96, :], in_=g.d["kpeT"]), writes=[b_kpe])
        c.op("pool", lambda e: e.memset(vaug[:, :, 64:65], 1.0), writes=[b_vaug])

        for t in range(16):
            ts = slice(t * 128, (t + 1) * 128)
            mm(c, pA[:, 0:256], [(h1o[:, k, ts], wq[:, k, :]) for k in range(8)], [b_h1o, b_w], b_pA)
            rms_tile(g, pA[:, 0:256], b_pA, sq, b_sq, rs, b_st, qg, b_tab, qn[:], b_qn)
            c.chain("pe", [lambda e, k=k: e.transpose(out=pT[:, k, :], in_=qn[:, k * 128:(k + 1) * 128],
                                                      identity=g.ident[:]) for k in range(2)],
                    reads=[b_qn, g.b_const], writes=[b_pT])
            c.op("dve", lambda e: e.tensor_copy(out=qnT[:, :, ts], in_=pT[:, 0:2, :]), reads=[b_pT], writes=[b_qnT])
            mm(c, pS[0][:], [(qnT[:, k, ts], wuq[:, k, 0:512]) for k in range(2)], [b_qnT, b_w], b_pS[0])
            mm(c, pS[1][:, 0:256], [(qnT[:, k, ts], wuq[:, k, 512:768]) for k in range(2)], [b_qnT, b_w], b_pS[1])
            qf2 = qf[:].rearrange("p h d -> p (h d)")
            c.op("act", lambda e: e.activation(out=qf2[:, 0:512], in_=pS[0][:], func=AF.Copy),
                 reads=[b_pS[0]], writes=[b_qf])
            c.op("act", lambda e: e.activation(out=qf2[:, 512:768], in_=pS[1][:, 0:256], func=AF.Copy),
                 reads=[b_pS[1]], writes=[b_qf])
            c.op("act", lambda e: e.activation(out=qc[:, :, 0:64], in_=qf[:, :, 0:64], func=AF.Copy),
                 reads=[b_qf], writes=[b_qc])
            rope_tile(g, qf[:, :, 64:80], qf[:, :, 80:96], b_qf,
                      cosq[:, t:t + 1, :].broadcast_to([128, 8, 16]), sinq[:, t:t + 1, :].broadcast_to([128, 8, 16]),
                      b_tab, rt, b_rt, qc[:, :, 64:80], qc[:, :, 80:96], b_qc)
            c.chain("pe", [lambda e, h=h: e.transpose(out=pT[0:96, h, :], in_=qc[:, h, :], identity=g.ident[:])
                           for h in range(8)], reads=[b_qc, g.b_const], writes=[b_pT])
            c.op("dve", lambda e: e.tensor_copy(out=qcT[:, :, ts], in_=pT[0:96, :, :]), reads=[b_pT], writes=[b_qcT])

        n = 0
        for h in range(8):
            for kb in range(16):
                ks = slice(kb * 512, (kb + 1) * 512)
                pp, bp = (pA, b_pA) if kb % 2 == 0 else (pB, b_pB)
                mm(c, pp[0:64, :], [(wukv[:, k, h * 128:h * 128 + 64], knT[:, k, ks]) for k in range(2)],
                   [b_w, b_knT], bp)
                eng = "act" if kb % 2 == 0 else "dve"
                if eng == "act":
                    c.op("act", lambda e: e.activation(out=kcat[0:64, ks], in_=pp[0:64, :], func=AF.Copy),
                         reads=[bp], writes=[b_kn[kb]])
                else:
                    c.op("dve", lambda e: e.tensor_copy(out=kcat[0:64, ks], in_=pp[0:64, :]),
                         reads=[bp], writes=[b_kn[kb]])
            for kg in range(8):
                fns = []
                for j in range(8):
                    kt = kg * 8 + j
                    for k in range(2):
                        fns.append(lambda e, j=j, k=k, kt=kt: e.matmul(
                            pV[:, j, :], lhsT=knT[:, k, kt * 128:(kt + 1) * 128],
                            rhs=wukv[:, k, h * 128 + 64:h * 128 + 128], start=(k == 0), stop=(k == 1)))
                c.chain("pe", fns, reads=[b_w, b_knT], writes=[b_pV])
                c.op("dve", lambda e: e.tensor_copy(out=vaug[:, kg * 8:(kg + 1) * 8, 0:64], in_=pV[:]),
                     reads=[b_pV], writes=[b_vaug])
            for qb in range(4):
                qs = slice(qb * 512, (qb + 1) * 512)
                for kt in range(64):
                    si = n % 3
                    n += 1
                    c.op("pe", lambda e: e.matmul(pS[si][:], lhsT=kcat[0:96, kt * 128:(kt + 1) * 128],
                                                  rhs=qcT[0:96, h, qs], start=True, stop=True),
                         reads=[b_kn[kt // 4], b_kpe, b_qcT], writes=[b_pS[si]])
                    c.op("act", lambda e: e.activation(out=pt[si][:], in_=pS[si][:], func=AF.Exp, scale=MLA_SCALE),
                         reads=[b_pS[si]], writes=[b_pt[si]])
                    c.op("pe", lambda e: e.matmul(pO[0:65, :], lhsT=vaug[:, kt, 0:65], rhs=pt[si][:],
                                                  start=(kt == 0), stop=(kt == 63)),
                         reads=[b_pt[si], b_vaug], writes=[b_pO])
                oi = qb % 2
                c.op("dve", lambda e: e.reciprocal(out=rrow[64:65, :], in_=pO[64:65, :]), reads=[b_pO], writes=[b_rrow])
                c.op("pe", lambda e: e.matmul(pB[0:64, :], lhsT=g.onesf[64:65, 0:64], rhs=rrow[64:65, :],
                                              start=True, stop=True), reads=[b_rrow, g.b_const], writes=[b_pB])
                c.op("act", lambda e: e.activation(out=bc[:], in_=pB[0:64, :], func=AF.Copy), reads=[b_pB], writes=[b_bc])
                c.op("dve", lambda e: e.tensor_tensor(out=ob[oi][:], in0=pO[0:64, :], in1=bc[:], op=ALU.mult),
                     reads=[b_pO, b_bc], writes=[b_ob[oi]])
                c.dma("sp", lambda e: e.dma_start(out=g.d["omla"][h * 64:(h + 1) * 64, qs], in_=ob[oi][:]),
                      reads=[b_ob[oi]], writes=[g.b_omla])
        c.barrier()


def dil_phase(g):
    nc, c = g.nc, g.c
    with ExitStack() as es:
        def sb(name, shape, dt):
            return es.enter_context(nc.sbuf_tensor("d_" + name, shape, dt))

        def ps(name, shape, dt):
            return es.enter_context(nc.psum_tensor("d_" + name, shape, dt))

        h1e = sb("h1e", [128, 8, 4096], BF16); b_h1e = Buf()
        etab = sb("etab", [128, 12, 2, 128], F32); vld = sb("vld", [128, 69], F32); b_tab = Buf()
        numT = sb("numT", [128, NOWN], F32); denT = sb("denT", [1, NOWN], F32); b_acc = Buf()
        qd = sb("qd", [128, NOWN], BF16); b_qd = Buf()
        kd = sb("kd", [128, 4096], BF16); b_kd = Buf()
        va = sb("va", [128, 32, 129], BF16); b_va = Buf()
        wqkv = [sb("wqkv%d" % i, [128, 8, 3, 128], BF16) for i in range(2)]; b_wqkv = [Buf(), Buf()]
        ex = [sb("ex%d" % i, [128, 128], F32) for i in range(2)]; b_ex = [Buf(), Buf()]
        p2 = [sb("p2%d" % i, [128, 128], BF16) for i in range(2)]; b_p2 = [Buf(), Buf()]
        bc = sb("bc", [128, 512], F32); b_bc = Buf()
        ob = [sb("ob%d" % i, [128, 512], BF16) for i in range(2)]; b_ob = [Buf(), Buf()]
        rrow = sb("rrow", [1, 512], F32); b_rrow = Buf()
        memb = sb("memb", [128, 2, D], BF16); b_memb = Buf()
        memT = sb("memT", [128, 8, 256], BF16); b_memT = Buf()
        wmkv = sb("wmkv", [128, 8, D], BF16); wmq = sb("wmq", [128, 8, 512], BF16); b_wm = Buf()
        kmT = sb("kmT", [128, 4, 256], BF16); vm = sb("vm", [128, 2, 512], BF16); b_km = Buf()
        qm = sb("qm", [128, 512], BF16); b_qm = Buf()
        ptm = [sb("ptm%d" % i, [128, 512], BF16) for i in range(2)]; b_ptm = [Buf(), Buf()]
        pA = ps("pA", [128, 512], F32); b_pA = Buf()
        pB = ps("pB", [128, 512], F32); b_pB = Buf()
        pV = [ps("pV%d" % i, [128, 512], F32) for i in range(2)]; b_pV = [Buf(), Buf()]
        pS = [ps("pS%d" % i, [128, 512], F32) for i in range(2)]; b_pS = [Buf(), Buf()]
        pN = ps("pN", [128, 512], F32); b_pN = Buf()
        pT = ps("pT", [128, 8, 128], BF16); b_pT = Buf()

        h1v = g.d["h1T"].rearrange("(k p) n -> p k n", p=128)
        for k in range(8):
            c.dma("sp", lambda e: e.dma_start(out=h1e[:, k, :], in_=h1v[:, k, :]), writes=[b_h1e])
        c.dma("sp", [lambda e: e.dma_start(out=etab[:].rearrange("p a b c -> p (a b c)"), in_=g.d["etab"]),
                     lambda e: e.dma_start(out=vld[:], in_=g.d["vld"])], writes=[b_tab])

        def load_wqkv(idx):
            hh, gi = idx // 3, idx % 3
            i = idx % 2
            fns = []
            for qkv in range(3):
                col = C_DIL + ((qkv * 3 + gi) * 4 + hh) * 128
                fns.append(lambda e, qkv=qkv, col=col: e.dma_start(
                    out=wqkv[i][:, :, qkv, :], in_=g.d["w_in"][:, col:col + 128].rearrange("(k p) n -> p k n", p=128)))
            c.dma("pool", fns, writes=[b_wqkv[i]])

        load_wqkv(0)
        c.dma("pool", [lambda e: e.dma_start(out=memb[:], in_=g.d["mem"].rearrange("(t p) n -> p t n", p=128)),
                       lambda e: e.dma_start(out=wmq[:], in_=g.d["w_in"][:, C_MEMQ:C_MEMQ + 512].rearrange(
                           "(k p) n -> p k n", p=128))], writes=[b_memb, b_wm])
        load_w(c, wmkv, b_wm, g.d["w_mem_kv"], 8)

        base_t = [0, 17, 37]
        cnt = 0
        for hh in range(4):
            for gi, (win, d) in enumerate(DIL):
                idx = hh * 3 + gi
                wi = idx % 2
                if idx + 1 < 12:
                    load_wqkv(idx + 1)
                w3 = wqkv[wi]
                hd = gi * 4 + hh
                nt = (NOWN // d) // 128 + 1
                for qb in range(4):
                    cs = slice(1024 + qb * 512, 1024 + (qb + 1) * 512)
                    mm(c, pA[:], [(w3[:, k, 0, :], h1e[:, k, cs]) for k in range(8)], [b_wqkv[wi], b_h1e], b_pA)
                    c.op("act", lambda e: e.activation(out=qd[:, qb * 512:(qb + 1) * 512], in_=pA[:], func=AF.Copy),
                         reads=[b_pA], writes=[b_qd])
                lo = (1024 - 64 * d) // 512
                hi = -(-(3072 + 64 * d) // 512)
                for eb in range(lo, hi):
                    cs = slice(eb * 512, (eb + 1) * 512)
                    mm(c, pB[:], [(w3[:, k, 1, :], h1e[:, k, cs]) for k in range(8)], [b_wqkv[wi], b_h1e], b_pB)
                    c.op("dve", lambda e: e.tensor_copy(out=kd[:, cs], in_=pB[:]), reads=[b_pB], writes=[b_kd])
                for cl in range(d):
                    for m in range(nt):
                        ti = cl * nt + m
                        st0 = 1024 + cl + d * (128 * m - 64)
                        ksl = slice(st0, st0 + 127 * d + 1, d)
                        vi = cnt % 2
                        cnt += 1
                        mm(c, pV[vi][:, 0:128], [(h1e[:, k, ksl], w3[:, k, 2, :]) for k in range(8)],
                           [b_wqkv[wi], b_h1e], b_pV[vi])
                        col = base_t[gi] + ti
                        c.op("dve", lambda e: e.tensor_scalar(out=va[:, ti, 0:128], in0=pV[vi][:, 0:128],
                                                              scalar1=vld[:, col:col + 1], scalar2=None, op0=ALU.mult),
                             reads=[b_pV[vi], b_tab], writes=[b_va])
                        c.op("act", lambda e: e.activation(out=va[:, ti, 128:129], in_=vld[:, col:col + 1], func=AF.Copy),
                             reads=[b_tab], writes=[b_va])
                nq = (NOWN // d) // 128
                for cl in range(d):
                    for mq in range(nq):
                        q0 = cl + d * 128 * mq
                        qsl = slice(q0, q0 + 127 * d + 1, d)
                        for side in range(2):
                            st0 = 1024 + cl + d * (128 * (mq + side) - 64)
                            ksl = slice(st0, st0 + 127 * d + 1, d)
                            c.op("pe", lambda e: e.matmul(pS[side][:, 0:128], lhsT=kd[:, ksl], rhs=qd[:, qsl],
                                                          start=True, stop=True),
                                 reads=[b_kd, b_qd], writes=[b_pS[side]])
                            c.op("act", lambda e: e.activation(out=ex[side][:], in_=pS[side][:, 0:128], func=AF.Exp,
                                                               scale=DH_SCALE), reads=[b_pS[side]], writes=[b_ex[side]])
                            c.op("dve", lambda e: e.tensor_tensor(out=p2[side][:], in0=ex[side][:],
                                                                  in1=etab[:, hd, side, :], op=ALU.mult),
                                 reads=[b_ex[side], b_tab], writes=[b_p2[side]])
                        tis = [cl * nt + mq, cl * nt + mq + 1]
                        c.chain("pe", [lambda e, s_=s_: e.matmul(pN[:, 0:128], lhsT=va[:, tis[s_], 0:128], rhs=p2[s_][:],
                                                                 start=(s_ == 0), stop=(s_ == 1)) for s_ in range(2)] +
                                [lambda e, s_=s_: e.matmul(pN[0:1, 128:256], lhsT=va[:, tis[s_], 128:129], rhs=p2[s_][:],
                                                           start=(s_ == 0), stop=(s_ == 1)) for s_ in range(2)],
                                reads=[b_va, b_p2[0], b_p2[1]], writes=[b_pN])
                        if gi == 0:
                            c.op("act", lambda e: e.activation(out=numT[:, qsl], in_=pN[:, 0:128], func=AF.Copy),
                                 reads=[b_pN], writes=[b_acc])
                            c.op("act", lambda e: e.activation(out=denT[0:1, qsl], in_=pN[0:1, 128:256], func=AF.Copy),
                                 reads=[b_pN], writes=[b_acc])
                        else:
                            c.op("dve", lambda e: e.tensor_tensor(out=numT[:, qsl], in0=pN[:, 0:128], in1=numT[:, qsl],
                                                                  op=ALU.add), reads=[b_pN, b_acc], writes=[b_acc])
                            c.op("dve", lambda e: e.tensor_tensor(out=denT[0:1, qsl], in0=pN[0:1, 128:256],
                                                                  in1=denT[0:1, qsl], op=ALU.add),
                                 reads=[b_pN, b_acc], writes=[b_acc])
            for qb in range(4):
                qs = slice(qb * 512, (qb + 1) * 512)
                oi = qb % 2
                c.op("dve", lambda e: e.reciprocal(out=rrow[:], in_=denT[0:1, qs]), reads=[b_acc], writes=[b_rrow])
                c.op("pe", lambda e: e.matmul(pA[:], lhsT=g.onesf[0:1, :], rhs=rrow[:], start=True, stop=True),
                     reads=[b_rrow, g.b_const], writes=[b_pA])
                c.op("act", lambda e: e.activation(out=bc[:], in_=pA[:], func=AF.Copy), reads=[b_pA], writes=[b_bc])
                c.op("dve", lambda e: e.tensor_tensor(out=ob[oi][:], in0=numT[:, qs], in1=bc[:], op=ALU.mult),
                     reads=[b_acc, b_bc], writes=[b_ob[oi]])
                c.dma("sp", lambda e: e.dma_start(out=g.d["odil"][hh * 128:(hh + 1) * 128, qs], in_=ob[oi][:]),
                      reads=[b_ob[oi]], writes=[g.b_odil])

        for t in range(2):
            c.chain("pe", [lambda e, k=k: e.transpose(out=pT[:, k, :], in_=memb[:, t, k * 128:(k + 1) * 128],
                                                      identity=g.ident[:]) for k in range(8)],
                    reads=[b_memb, g.b_const], writes=[b_pT])
            c.op("dve", lambda e: e.tensor_copy(out=memT[:, :, t * 128:(t + 1) * 128], in_=pT[:]),
                 reads=[b_pT], writes=[b_memT])
        for h in range(4):
            mm(c, pA[:, 0:256], [(wmkv[:, k, h * 128:(h + 1) * 128], memT[:, k, :]) for k in range(8)],
               [b_wm, b_memT], b_pA)
            c.op("act", lambda e: e.activation(out=kmT[:, h, :], in_=pA[:, 0:256], func=AF.Copy),
                 reads=[b_pA], writes=[b_km])
        for kt in range(2):
            mm(c, pB[:], [(memT[:, k, kt * 128:(kt + 1) * 128], wmkv[:, k, 512:1024]) for k in range(8)],
               [b_wm, b_memT], b_pB)
            c.op("dve", lambda e: e.tensor_copy(out=vm[:, kt, :], in_=pB[:]), reads=[b_pB], writes=[b_km])
        for h in range(4):
            for qb in range(4):
                qs = slice(qb * 512, (qb + 1) * 512)
                cs = slice(1024 + qb * 512, 1024 + (qb + 1) * 512)
                oi = qb % 2
                mm(c, pB[:], [(wmq[:, k, h * 128:(h + 1) * 128], h1e[:, k, cs]) for k in range(8)], [b_wm, b_h1e], b_pB)
                c.op("dve", lambda e: e.tensor_copy(out=qm[:], in_=pB[:]), reads=[b_pB], writes=[b_qm])
                for kt in range(2):
                    c.op("pe", lambda e: e.matmul(pS[kt][:], lhsT=kmT[:, h, kt * 128:(kt + 1) * 128], rhs=qm[:],
                                                  start=True, stop=True), reads=[b_km, b_qm], writes=[b_pS[kt]])
                    c.op("act", lambda e: e.activation(out=ptm[kt][:], in_=pS[kt][:], func=AF.Exp, scale=DH_SCALE),
                         reads=[b_pS[kt]], writes=[b_ptm[kt]])
                c.chain("pe", [lambda e, kt=kt: e.matmul(pV[0][:], lhsT=vm[:, kt, h * 128:(h + 1) * 128], rhs=ptm[kt][:],
                                                         start=(kt == 0), stop=(kt == 1)) for kt in range(2)] +
                        [lambda e, kt=kt: e.matmul(pV[1][0:1, :], lhsT=g.onesb[:, 0:1], rhs=ptm[kt][:],
                                                   start=(kt == 0), stop=(kt == 1)) for kt in range(2)],
                        reads=[b_km, b_ptm[0], b_ptm[1], g.b_const], writes=[b_pV[0], b_pV[1]])
                c.op("dve", lambda e: e.reciprocal(out=rrow[:], in_=pV[1][0:1, :]), reads=[b_pV[1]], writes=[b_rrow])
                c.op("pe", lambda e: e.matmul(pA[:], lhsT=g.onesf[0:1, :], rhs=rrow[:], start=True, stop=True),
                     reads=[b_rrow, g.b_const], writes=[b_pA])
                c.op("act", lambda e: e.activation(out=bc[:], in_=pA[:], func=AF.Copy), reads=[b_pA], writes=[b_bc])
                c.op("dve", lambda e: e.tensor_tensor(out=ob[oi][:], in0=pV[0][:], in1=bc[:], op=ALU.mult),
                     reads=[b_pV[0], b_bc], writes=[b_ob[oi]])
                c.dma("sp", lambda e: e.dma_start(out=g.d["omem"][h * 128:(h + 1) * 128, qs], in_=ob[oi][:]),
                      reads=[b_ob[oi]], writes=[g.b_omem])
        c.barrier()


def merge_phase(g):
    nc, c = g.nc, g.c
    with ExitStack() as es:
        def sb(name, shape, dt):
            return es.enter_context(nc.sbuf_tensor("g_" + name, shape, dt))

        def ps(name, shape, dt):
            return es.enter_context(nc.psum_tensor("g_" + name, shape, dt))

        wgate = sb("wgate", [128, 8, 3072], BF16); b_wgate = Buf()
        wbm = sb("wbm", [64, 8, D], BF16); wbd = sb("wbd", [128, 4, D], BF16); wbe = sb("wbe", [128, 4, D], BF16)
        wo = sb("wo", [128, 8, D], BF16); b_w = Buf()
        lng = sb("lng", [128, D], F32); lnb = sb("lnb", [128, D], F32); b_ln = Buf()
        h1b = [sb("h1b%d" % i, [128, 8, 512], BF16) for i in range(2)]
        omb = [sb("omb%d" % i, [64, 8, 512], BF16) for i in range(2)]
        odb = [sb("odb%d" % i, [128, 4, 512], BF16) for i in range(2)]
        oeb = [sb("oeb%d" % i, [128, 4, 512], BF16) for i in range(2)]
        b_in = [Buf(), Buf()]
        sg = [sb("sg%d" % i, [128, 512], F32) for i in range(2)]; b_sg = [Buf(), Buf()]
        tmp = [sb("tmp%d" % i, [128, 512], F32) for i in range(2)]; b_tmp = [Buf(), Buf()]
        acc = [sb("acc%d" % i, [128, 512], F32) for i in range(2)]; b_accs = [Buf(), Buf()]
        mT = sb("mT", [128, 8, 512], BF16); b_mT = Buf()
        xr = [sb("xr%d" % i, [128, D], F32) for i in range(2)]; b_xr = [Buf(), Buf()]
        z = sb("z", [128, D], F32); b_z = Buf()
        yn = [sb("yn%d" % i, [128, D], F32) for i in range(2)]; b_yn = [Buf(), Buf()]
        st = sb("st", [128, 2, 6], F32); mv = sb("mv", [128, 2], F32); rs = sb("rs", [128, 4], F32); b_st = Buf()
        pY = [ps("pY%d" % i, [128, 512], F32) for i in range(2)]; b_pY = [Buf(), Buf()]
        pG = [ps("pG%d" % i, [128, 512], F32) for i in range(2)]; b_pG = [Buf(), Buf()]
        pZ = [ps("pZ%d" % i, [128, 512], F32) for i in range(2)]; b_pZ = [Buf(), Buf()]

        h1v = g.d["h1T"].rearrange("(k p) n -> p k n", p=128)
        omv = g.d["omla"].rearrange("(h p) n -> p h n", p=64)
        odv = g.d["odil"].rearrange("(h p) n -> p h n", p=128)
        oev = g.d["omem"].rearrange("(h p) n -> p h n", p=128)

        def load_blk(qb):
            i = qb % 2
            qs = slice(qb * 512, (qb + 1) * 512)
            c.dma("sp", [lambda e: e.dma_start(out=h1b[i][:], in_=h1v[:, :, 1024 + qb * 512:1024 + (qb + 1) * 512]),
                         lambda e: e.dma_start(out=omb[i][:], in_=omv[:, :, qs]),
                         lambda e: e.dma_start(out=odb[i][:], in_=odv[:, :, qs]),
                         lambda e: e.dma_start(out=oeb[i][:], in_=oev[:, :, qs])], writes=[b_in[i]])

        def load_xr(gt):
            i = gt % 2
            c.dma("sp", lambda e: e.dma_start(out=xr[i][:], in_=g.d["h1own"][gt * 128:(gt + 1) * 128, :]),
                  writes=[b_xr[i]])

        load_blk(0)
        c.dma("sp", [lambda e: e.dma_start(out=lng[:], in_=g.d["lnp"][2]),
                     lambda e: e.dma_start(out=lnb[:], in_=g.d["lnp"][3])], writes=[b_ln])
        c.dma("pool", [lambda e: e.dma_start(out=wbm[:], in_=g.d["w_br_mla"].rearrange("(h p) n -> p h n", p=64)),
                       lambda e: e.dma_start(out=wbd[:], in_=g.d["w_br_dil"].rearrange("(h p) n -> p h n", p=128)),
                       lambda e: e.dma_start(out=wbe[:], in_=g.d["w_br_mem"].rearrange("(h p) n -> p h n", p=128))],
              writes=[b_w])
        for k in range(8):
            c.dma("pool", lambda e: e.dma_start(out=wgate[:, k, :], in_=g.d["w_in"][k * 128:(k + 1) * 128, C_GATE:C_GATE + 3072]),
                  writes=[b_wgate])
        load_w(c, wo, b_w, g.d["w_o"], 8)
        load_xr(0)
        n = 0
        for qb in range(4):
            i = qb % 2
            if qb + 1 < 4:
                load_blk(qb + 1)
            for fc in range(8):
                fs = slice(fc * 128, (fc + 1) * 128)
                ai = fc % 2
                for br in range(3):
                    j = n % 2
                    n += 1
                    if br == 0:
                        pairs = [(wbm[0:64, h, fs], omb[i][0:64, h, :]) for h in range(8)]
                    elif br == 1:
                        pairs = [(wbd[:, h, fs], odb[i][:, h, :]) for h in range(4)]
                    else:
                        pairs = [(wbe[:, h, fs], oeb[i][:, h, :]) for h in range(4)]
                    mm(c, pY[j][:], pairs, [b_w, b_in[i]], b_pY[j])
                    gc = br * 1024 + fc * 128
                    mm(c, pG[j][:], [(wgate[:, k, gc:gc + 128], h1b[i][:, k, :]) for k in range(8)],
                       [b_wgate, b_in[i]], b_pG[j])
                    c.op("act", lambda e: e.activation(out=sg[j][:], in_=pG[j][:], func=AF.Sigmoid),
                         reads=[b_pG[j]], writes=[b_sg[j]])
                    if br == 0:
                        c.op("dve", lambda e: e.tensor_tensor(out=acc[ai][:], in0=sg[j][:], in1=pY[j][:], op=ALU.mult),
                             reads=[b_sg[j], b_pY[j]], writes=[b_accs[ai]])
                    else:
                        c.op("dve", lambda e: e.tensor_tensor(out=tmp[j][:], in0=sg[j][:], in1=pY[j][:], op=ALU.mult),
                             reads=[b_sg[j], b_pY[j]], writes=[b_tmp[j]])
                        if br == 1:
                            c.op("pool", lambda e: e.tensor_tensor(out=acc[ai][:], in0=acc[ai][:], in1=tmp[j][:],
                                                                   op=ALU.add),
                                 reads=[b_accs[ai], b_tmp[j]], writes=[b_accs[ai]])
                        else:
                            c.op("pool", lambda e: e.tensor_tensor(out=mT[:, fc, :], in0=acc[ai][:], in1=tmp[j][:],
                                                                   op=ALU.add),
                                 reads=[b_accs[ai], b_tmp[j]], writes=[b_mT])
            for t in range(4):
                gt = qb * 4 + t
                ri = gt % 2
                if gt + 1 < 16:
                    load_xr(gt + 1)
                ts = slice(t * 128, (t + 1) * 128)
                for h in range(2):
                    mm(c, pZ[h][:], [(mT[:, fc, ts], wo[:, fc, h * 512:(h + 1) * 512]) for fc in range(8)],
                       [b_mT, b_w], b_pZ[h])
                c.op("act", lambda e: e.activation(out=z[:], in_=xr[ri][:], func=AF.Copy, scale=ALPHA),
                     reads=[b_xr[ri]], writes=[b_z])
                for h in range(2):
                    hs = slice(h * 512, (h + 1) * 512)
                    c.op("dve", lambda e: e.tensor_tensor(out=z[:, hs], in0=pZ[h][:], in1=z[:, hs], op=ALU.add),
                         reads=[b_pZ[h], b_z], writes=[b_z])
                layer_norm_tile(g, z, b_z, yn[ri], b_yn[ri], st, mv, rs, b_st, lng, lnb, b_ln)
                c.dma("sp", lambda e: e.dma_start(out=g.d["h2"][gt * 128:(gt + 1) * 128, :], in_=yn[ri][:]),
                      reads=[b_yn[ri]], writes=[g.b_h2])
        c.barrier()


def build(stage=99, debug=False):
    nc = bass.Bass("TRN2", target_bir_lowering=False)
    g = K()
    g.nc = nc
    g.c = Ctx(nc)
    g.d = {}

    def din(name, shape, dt=F32):
        g.d[name] = nc.dram_tensor(name, shape, dt, kind="ExternalInput").ap()

    def dscr(name, shape, dt):
        g.d[name] = nc.dram_tensor(name, shape, dt, kind="ExternalOutput" if debug else "Internal").ap()

    din("xctx", [S, D]); din("mem", [256, D]); din("w_in", [D, 8736])
    din("qg", [128, 256]); din("kvg", [128, 256]); din("w_uq", [256, 768]); din("w_ukv", [256, 1024])
    din("w_mem_kv", [D, D]); din("w_br_mla", [512, D]); din("w_br_dil", [512, D]); din("w_br_mem", [512, D])
    din("w_o", [D, D])
    for f in ("f1", "f2"):
        din(f + "g", [D, FF]); din(f + "u", [D, FF]); din(f + "d", [FF, D])
    din("lnp", [6, 128, D]); din("cosk", [128, 64, 16]); din("sink", [128, 64, 16])
    din("vld", [128, 69]); din("etab", [128, 12 * 2 * 128]); din("identf", [128, 128])
    g.d["out"] = nc.dram_tensor("out", [NOWN, D], F32, kind="ExternalOutput").ap()
    dscr("h1own", [NOWN, D], F32); dscr("h1T", [D, 4096], BF16)
    dscr("knT", [256, S], BF16); dscr("kpeT", [32, S], BF16)
    dscr("omla", [512, NOWN], BF16); dscr("odil", [512, NOWN], BF16); dscr("omem", [512, NOWN], BF16)
    dscr("h2", [NOWN, D], F32)
    for n in ("h1own", "h1T", "knT", "omla", "odil", "omem", "h2", "out"):
        setattr(g, "b_" + n, Buf(n))
    c = g.c
    with ExitStack() as es:
        g.ident = es.enter_context(nc.sbuf_tensor("ident", [128, 128], BF16))
        g.onesb = es.enter_context(nc.sbuf_tensor("onesb", [128, 128], BF16))
        g.onesf = es.enter_context(nc.sbuf_tensor("onesf", [128, 128], F32))
        idf = es.enter_context(nc.sbuf_tensor("idf", [128, 128], F32))
        g.b_const = Buf("const")
        c.dma("sp", lambda e: e.dma_start(out=idf[:], in_=g.d["identf"]), writes=[g.b_const])
        c.op("dve", lambda e: e.tensor_copy(out=g.ident[:], in_=idf[:]), reads=[g.b_const], writes=[g.b_const])
        c.op("pool", lambda e: e.memset(g.onesb[:], 1.0), writes=[g.b_const])
        c.op("pool", lambda e: e.memset(g.onesf[:], 1.0), writes=[g.b_const])
        g.epsc = es.enter_context(nc.sbuf_tensor("epsc", [128, 1], F32))
        c.op("pool", lambda e: e.memset(g.epsc[:], EPS), writes=[g.b_const])
        ffn_phase(g, g.d["xctx"], 32 if stage != 0 else 2, g.d["f1g"], g.d["f1u"], g.d["f1d"],
                  g.d["lnp"][0], g.d["lnp"][1], True, "a_")
        if stage >= 2:
            mla_phase(g)
        if stage >= 3:
            dil_phase(g)
        if stage >= 4:
            merge_phase(g)
        if stage >= 5:
            ffn_phase(g, g.d["h2"], 8, g.d["f2g"], g.d["f2u"], g.d["f2d"], g.d["lnp"][4], g.d["lnp"][5],
                      False, "f_")
        c.finish("sp")
    return nc


def host_tables(r):
    order = [r, (r + 3) % 4, (r + 1) % 4, (r + 2) % 4]
    pos = np.concatenate([np.arange(o * NOWN, (o + 1) * NOWN) for o in order]).astype(np.float32)
    inv = 1.0 / (10000.0 ** (np.arange(0, 32, 2, dtype=np.float32) / 32))
    ang = pos[:, None] * inv[None, :]
    cosk = np.cos(ang).astype(np.float32).reshape(64, 128, 16).transpose(1, 0, 2)
    sink = np.sin(ang).astype(np.float32).reshape(64, 128, 16).transpose(1, 0, 2)
    q0 = r * NOWN
    epos = q0 - 1024 + np.arange(4096)
    evalid = ((epos >= 0) & (epos < S)).astype(np.float32)
    cols = []
    for (win, d) in DIL:
        nt = (NOWN // d) // 128 + 1
        for cl in range(d):
            for m in range(nt):
                sub = -64 + 128 * m + np.arange(128)
                e = 1024 + cl + d * sub
                cols.append(evalid[e])
    vld = np.stack(cols, 1).astype(np.float32)
    return order, np.ascontiguousarray(cosk), np.ascontiguousarray(sink), vld


def etab_table():
    slopes = 2.0 ** (-8.0 * np.arange(1, 13, dtype=np.float32) / 12)
    i = np.arange(128)[:, None]
    j = np.arange(128)[None, :]
    tab = np.zeros((128, 12, 2, 128), np.float32)
    for gi, (win, d) in enumerate(DIL):
        for hh in range(4):
            hd = gi * 4 + hh
            for side, off in enumerate((-64, 64)):
                rel = i + off - j
                tab[:, hd, side, :] = np.where(np.abs(rel) <= 64, np.exp(-slopes[hd] * d * np.abs(rel)), 0.0)
    return np.ascontiguousarray(tab.reshape(128, -1))


def make_in_maps(inp):
    f = lambda a: np.ascontiguousarray(np.asarray(a, dtype=np.float32))
    x = f(inp["x"]); mem = f(inp["mem"])
    rep = lambda v, n: np.ascontiguousarray(np.broadcast_to(f(v).reshape(1, n), (128, n)))
    common = {
        "w_in": f(inp["w_in"])[0], "qg": rep(inp["mla_q_norm"], 256), "kvg": rep(inp["mla_kv_norm"], 256),
        "w_uq": f(inp["w_uq"])[0], "w_ukv": f(inp["w_ukv"])[0], "w_mem_kv": f(inp["w_mem_kv"])[0],
        "w_br_mla": f(inp["w_br_mla"])[0], "w_br_dil": f(inp["w_br_dil"])[0], "w_br_mem": f(inp["w_br_mem"])[0],
        "w_o": f(inp["w_o"])[0],
        "f1g": f(inp["ffn1_w_gate"])[0], "f1u": f(inp["ffn1_w_up"])[0], "f1d": f(inp["ffn1_w_down"])[0],
        "f2g": f(inp["ffn2_w_gate"])[0], "f2u": f(inp["ffn2_w_up"])[0], "f2d": f(inp["ffn2_w_down"])[0],
        "lnp": np.ascontiguousarray(np.stack([rep(inp[k], D) for k in
                                              ("ln1_g", "ln1_b", "ln2_g", "ln2_b", "ln3_g", "ln3_b")], 0)),
        "etab": etab_table(), "identf": np.eye(128, dtype=np.float32),
    }
    maps = []
    for core in range(8):
        b, r = core // 4, core % 4
        order, cosk, sink, vld = host_tables(r)
        xctx = np.ascontiguousarray(np.concatenate([x[b, o * NOWN:(o + 1) * NOWN] for o in order], 0))
        m = dict(common)
        m.update({"xctx": xctx, "mem": np.ascontiguousarray(mem[b]), "cosk": cosk, "sink": sink, "vld": vld})
        maps.append(m)
    return maps


_NC_CACHE = {}


def kernel(**inputs):
    if "nc" not in _NC_CACHE:
        _NC_CACHE["nc"] = build()
    nc = _NC_CACHE["nc"]
    maps = make_in_maps(inputs)
    res = run_bass_kernel_spmd(nc, maps, core_ids=list(range(8)))
    out = np.zeros((2, S, D), np.float32)
    for core in range(8):
        b, r = core // 4, core % 4
        out[b, r * NOWN:(r + 1) * NOWN] = res.results[core]["out"]
    return out
```

```python
import numpy as np
import ml_dtypes
from contextlib import ExitStack
import concourse.bass as bass
import concourse.mybir as mybir
from concourse.bass_utils import run_bass_kernel_spmd

F32 = mybir.dt.float32
BF16 = mybir.dt.bfloat16
AF = mybir.ActivationFunctionType
ALU = mybir.AluOpType
AX = mybir.AxisListType

D = 1024
S = 8192
NOWN = 2048
FF = 2816
NFC = 22
EPS = 1e-5
ALPHA = 2.0 ** 0.25
C_CQ, C_CKV, C_KR, C_DIL, C_MEMQ, C_GATE = 0, 256, 512, 544, 5152, 5664
MLA_SCALE = 96.0 ** -0.5
DH_SCALE = 128.0 ** -0.5
DIL = ((128, 1), (512, 4), (2048, 16))


class Buf:
    __slots__ = ("name", "w", "r")

    def __init__(self, name=""):
        self.name = name
        self.w = None
        self.r = []


class Ctx:
    NDMASEM = 48

    def __init__(self, nc):
        self.nc = nc
        self.eng = {"pe": nc.tensor, "act": nc.scalar, "dve": nc.vector,
                    "pool": nc.gpsimd, "sp": nc.sync}
        self.sem = {}
        self.cnt = {}
        for e in ("pe", "act", "dve", "pool"):
            self.sem[e] = nc.alloc_semaphore("c_" + e)
            self.cnt[e] = 0
        self.dsem = [nc.alloc_semaphore("d%d" % i) for i in range(self.NDMASEM)]
        self.dcnt = [0] * self.NDMASEM
        self.dpool = {"sp": list(range(0, 32)), "pool": list(range(32, self.NDMASEM))}
        self.dnext = {"sp": 0, "pool": 0}
        self.seen = {e: {} for e in self.eng}

    def _wait(self, eng, tok):
        key, val = tok
        if key == "pe" and eng == "pe":
            return
        s = self.seen[eng]
        if s.get(key, 0) >= val:
            return
        s[key] = val
        sem = self.sem[key] if isinstance(key, str) else self.dsem[key]
        self.eng[eng].wait_ge(sem, val)

    def _deps(self, eng, reads, writes):
        for b in reads:
            if b.w is not None:
                self._wait(eng, b.w)
        for b in writes:
            if b.w is not None:
                self._wait(eng, b.w)
            for t in b.r:
                self._wait(eng, t)

    def _commit(self, tok, reads, writes):
        for b in reads:
            b.r.append(tok)
            if len(b.r) > 48:
                d = {}
                for k, v in b.r:
                    if d.get(k, 0) < v:
                        d[k] = v
                b.r = list(d.items())
        for b in writes:
            b.w = tok
            b.r = []

    def op(self, eng, fn, reads=(), writes=()):
        self._deps(eng, reads, writes)
        ins = fn(self.eng[eng])
        self.cnt[eng] += 1
        ins.then_inc(self.sem[eng], 1)
        self._commit((eng, self.cnt[eng]), reads, writes)
        return ins

    def chain(self, eng, fns, reads=(), writes=()):
        self._deps(eng, reads, writes)
        ins = None
        for fn in fns:
            ins = fn(self.eng[eng])
        self.cnt[eng] += 1
        ins.then_inc(self.sem[eng], 1)
        self._commit((eng, self.cnt[eng]), reads, writes)

    def dma(self, q, fns, reads=(), writes=()):
        if not isinstance(fns, (list, tuple)):
            fns = [fns]
        self._deps(q, reads, writes)
        pl = self.dpool[q]
        i = pl[self.dnext[q]]
        self.dnext[q] = (self.dnext[q] + 1) % len(pl)
        if self.dcnt[i] > 0:
            self._wait(q, (i, self.dcnt[i]))
        for fn in fns:
            ins = fn(self.eng[q])
            ins.then_inc(self.dsem[i], 16)
            self.dcnt[i] += 16
        self._commit((i, self.dcnt[i]), reads, writes)

    def all_tokens(self):
        toks = [(e, self.cnt[e]) for e in self.cnt if self.cnt[e] > 0]
        toks += [(i, n) for i, n in enumerate(self.dcnt) if n > 0]
        return toks

    def barrier(self):
        toks = self.all_tokens()
        for e in self.eng:
            for t in toks:
                self._wait(e, t)

    def finish(self, eng="sp"):
        for t in self.all_tokens():
            self._wait(eng, t)


def mm(c, out_ap, pairs, reads, out_buf):
    n = len(pairs)
    fns = []
    for i, (l, r) in enumerate(pairs):
        fns.append(lambda e, l=l, r=r, i=i: e.matmul(out_ap, lhsT=l, rhs=r, start=(i == 0), stop=(i == n - 1)))
    c.chain("pe", fns, reads=reads, writes=[out_buf])


class K:
    pass


def load_w(c, dst_tile, dst_buf, src, nk, q="pool"):
    for k in range(nk):
        c.dma(q, lambda e, k=k: e.dma_start(out=dst_tile[:, k, :], in_=src[k * 128:(k + 1) * 128, :]),
              writes=[dst_buf])


def ffn_phase(g, x_d, nblk, wg_d, wu_d, wd_d, lng_d, lnb_d, first, pfx):
    nc, c = g.nc, g.c
    with ExitStack() as es:
        def sb(name, shape, dt):
            return es.enter_context(nc.sbuf_tensor(pfx + name, shape, dt))

        def ps(name, shape, dt):
            return es.enter_context(nc.psum_tensor(pfx + name, shape, dt))

        TB = 256
        NT = TB // 128
        wg = sb("wg", [128, 8, FF], BF16); b_wg = Buf()
        wu = sb("wu", [128, 8, FF], BF16); b_wu = Buf()
        wd = sb("wd", [128, NFC, D], BF16); b_wd = Buf()
        lng = sb("lng", [128, D], F32); lnb = sb("lnb", [128, D], F32); b_ln = Buf()
        xb = sb("xb", [128, NT, D], BF16); b_xb = Buf()
        xr = [sb("xr%d" % i, [128, D], F32) for i in range(2)]; b_xr = [Buf(), Buf()]
        xT = [sb("xT%d" % i, [128, 8, TB], BF16) for i in range(2)]; b_xT = [Buf(), Buf()]
        hT = sb("hT", [128, NFC, TB], BF16); b_hT = [Buf() for _ in range(NFC)]
        sg = [sb("sg%d" % i, [128, TB], F32) for i in range(2)]; b_sg = [Buf(), Buf()]
        z = sb("z", [128, D], F32); b_z = Buf()
        yn = [sb("yn%d" % i, [128, D], F32) for i in range(2)]; b_yn = [Buf(), Buf()]
        st = sb("st", [128, 2, 6], F32); mv = sb("mv", [128, 2], F32)
        rs = sb("rs", [128, 4], F32); b_st = Buf()
        pGU = [ps("pGU%d" % i, [128, 2, TB], F32) for i in range(2)]; b_pGU = [Buf(), Buf()]
        pY = [[ps("pY%d%d" % (t, h), [128, 512], F32) for h in range(2)] for t in range(NT)]
        b_pY = [[Buf(), Buf()] for _ in range(NT)]
        pT = ps("pT", [128, 8, 128], BF16); b_pT = Buf()
        if first:
            pL = ps("pL", [128, 512], F32); b_pL = Buf()
            wlat = sb("wlat", [128, 8, 288], BF16); b_wlat = Buf()
            kvg = sb("kvg", [128, 256], F32); b_tab = Buf()
            cosk = [sb("cosk%d" % i, [128, NT, 16], F32) for i in range(3)]
            sink = [sb("sink%d" % i, [128, NT, 16], F32) for i in range(3)]
            b_cs = [Buf(), Buf(), Buf()]
            hb = sb("hb", [128, D], BF16); b_hb = Buf()
            h1T = sb("h1T", [128, 8, TB], BF16); b_h1T = Buf()
            sq = sb("sq", [128, 256], F32); b_sq = Buf()
            kn = sb("kn", [128, 256], BF16); kpe = sb("kpe", [128, 32], BF16); b_kn = Buf()
            rt = sb("rt", [128, 4, 16], F32)
            knTb = sb("knTb", [128, 2, TB], BF16); kpeTb = sb("kpeTb", [32, TB], BF16); b_kT = Buf()

        def load_x(b):
            c.dma("pool", lambda e: e.dma_start(
                out=xb[:], in_=x_d[b * TB:(b + 1) * TB, :].rearrange("(t p) n -> p t n", p=128)),
                writes=[b_xb])
            if first:
                i3 = b % 3
                c.dma("sp", [lambda e: e.dma_start(out=cosk[i3][:], in_=g.d["cosk"][:, b * NT:(b + 1) * NT, :]),
                             lambda e: e.dma_start(out=sink[i3][:], in_=g.d["sink"][:, b * NT:(b + 1) * NT, :])],
                      writes=[b_cs[i3]])

        def load_xr(gt):
            i = gt % 2
            c.dma("sp", lambda e: e.dma_start(out=xr[i][:], in_=x_d[gt * 128:(gt + 1) * 128, :]),
                  writes=[b_xr[i]])

        def xT_stage(b):
            i = b % 2
            for t in range(NT):
                c.chain("pe", [lambda e, k=k: e.transpose(out=pT[:, k, :], in_=xb[:, t, k * 128:(k + 1) * 128],
                                                          identity=g.ident[:]) for k in range(8)],
                        reads=[b_xb, g.b_const], writes=[b_pT])
                c.op("dve", lambda e: e.tensor_copy(out=xT[i][:, :, t * 128:(t + 1) * 128], in_=pT[:]),
                     reads=[b_pT], writes=[b_xT[i]])

        def phase1(b):
            i = b % 2
            for cc in range(NFC):
                j = cc % 2
                cs = slice(cc * 128, (cc + 1) * 128)
                fns = [lambda e, k=k: e.matmul(pGU[j][:, 0, :], lhsT=wg[:, k, cs], rhs=xT[i][:, k, :],
                                               start=(k == 0), stop=(k == 7)) for k in range(8)]
                fns += [lambda e, k=k: e.matmul(pGU[j][:, 1, :], lhsT=wu[:, k, cs], rhs=xT[i][:, k, :],
                                                start=(k == 0), stop=(k == 7)) for k in range(8)]
                c.chain("pe", fns, reads=[b_wg, b_wu, b_xT[i]], writes=[b_pGU[j]])
                c.op("act", lambda e: e.activation(out=sg[j][:], in_=pGU[j][:, 0, :], func=AF.Silu),
                     reads=[b_pGU[j]], writes=[b_sg[j]])
                c.op("dve", lambda e: e.tensor_tensor(out=hT[:, cc, :], in0=sg[j][:], in1=pGU[j][:, 1, :], op=ALU.mult),
                     reads=[b_sg[j], b_pGU[j]], writes=[b_hT[cc]])

        def phase2(b):
            for t in range(NT):
                ts = slice(t * 128, (t + 1) * 128)
                for h in range(2):
                    mm(c, pY[t][h][:], [(hT[:, cc, ts], wd[:, cc, h * 512:(h + 1) * 512]) for cc in range(NFC)],
                       b_hT + [b_wd], b_pY[t][h])
            for t in range(NT):
                gt = b * NT + t
                ri = gt % 2
                if gt + 1 < nblk * NT:
                    load_xr(gt + 1)
                c.op("act", lambda e: e.activation(out=z[:], in_=xr[ri][:], func=AF.Copy, scale=ALPHA),
                     reads=[b_xr[ri]], writes=[b_z])
                for h in range(2):
                    hs = slice(h * 512, (h + 1) * 512)
                    c.op("dve", lambda e: e.scalar_tensor_tensor(out=z[:, hs], in0=pY[t][h][:], scalar=0.5,
                                                                  in1=z[:, hs], op0=ALU.mult, op1=ALU.add),
                         reads=[b_pY[t][h], b_z], writes=[b_z])
                layer_norm_tile(g, z, b_z, yn[ri], b_yn[ri], st, mv, rs, b_st, lng, lnb, b_ln)
                if first:
                    if gt < 16:
                        c.dma("sp", lambda e: e.dma_start(out=g.d["h1own"][gt * 128:(gt + 1) * 128, :], in_=yn[ri][:]),
                              reads=[b_yn[ri]], writes=[g.b_h1own])
                else:
                    c.dma("sp", lambda e: e.dma_start(out=g.d["out"][gt * 128:(gt + 1) * 128, :], in_=yn[ri][:]),
                          reads=[b_yn[ri]], writes=[g.b_out])

        def post(b):
            i3 = b % 3
            for t in range(NT):
                gt = b * NT + t
                ri = gt % 2
                ts = slice(t * 128, (t + 1) * 128)
                c.op("act", lambda e: e.activation(out=hb[:], in_=yn[ri][:], func=AF.Copy),
                     reads=[b_yn[ri]], writes=[b_hb])
                c.chain("pe", [lambda e, k=k: e.transpose(out=pT[:, k, :], in_=hb[:, k * 128:(k + 1) * 128],
                                                          identity=g.ident[:]) for k in range(8)],
                        reads=[b_hb, g.b_const], writes=[b_pT])
                c.op("dve", lambda e: e.tensor_copy(out=h1T[:, :, ts], in_=pT[:]), reads=[b_pT], writes=[b_h1T])
                mm(c, pL[:, 0:288], [(h1T[:, k, ts], wlat[:, k, :]) for k in range(8)], [b_h1T, b_wlat], b_pL)
                rms_tile(g, pL[:, 0:256], b_pL, sq, b_sq, rs, b_st, kvg, b_tab, kn[:], b_kn)
                rope_tile(g, pL[:, 256:272], pL[:, 272:288], b_pL, cosk[i3][:, t, :], sink[i3][:, t, :], b_cs[i3],
                          rt, b_sq, kpe[:, 0:16], kpe[:, 16:32], b_kn)
                c.chain("pe", [lambda e, k=k: e.transpose(out=pT[:, k, :], in_=kn[:, k * 128:(k + 1) * 128],
                                                          identity=g.ident[:]) for k in range(2)] +
                        [lambda e: e.transpose(out=pT[0:32, 2, :], in_=kpe[:], identity=g.ident[:])],
                        reads=[b_kn, g.b_const], writes=[b_pT])
                c.op("dve", lambda e: e.tensor_copy(out=knTb[:, :, ts], in_=pT[:, 0:2, :]), reads=[b_pT], writes=[b_kT])
                c.op("dve", lambda e: e.tensor_copy(out=kpeTb[:, ts], in_=pT[0:32, 2, :]), reads=[b_pT], writes=[b_kT])
            bs = slice(b * TB, (b + 1) * TB)
            c.dma("sp", [lambda e: e.dma_start(out=g.d["knT"].rearrange("(k p) n -> p k n", p=128)[:, :, bs], in_=knTb[:]),
                         lambda e: e.dma_start(out=g.d["kpeT"][:, bs], in_=kpeTb[:])],
                  reads=[b_kT], writes=[g.b_knT])
            ext = None
            tok0 = b * TB
            if tok0 < 2048:
                ext = 1024 + tok0
            elif 3072 <= tok0 < 4096:
                ext = tok0 - 3072
            elif 4096 <= tok0 < 5120:
                ext = 3072 + (tok0 - 4096)
            if ext is not None:
                c.dma("sp", lambda e: e.dma_start(
                    out=g.d["h1T"].rearrange("(k p) n -> p k n", p=128)[:, :, ext:ext + TB], in_=h1T[:]),
                    reads=[b_h1T], writes=[g.b_h1T])

        load_x(0)
        load_xr(0)
        c.dma("sp", [lambda e: e.dma_start(out=lng[:], in_=lng_d), lambda e: e.dma_start(out=lnb[:], in_=lnb_d)],
              writes=[b_ln])
        if first:
            c.dma("sp", [lambda e: e.dma_start(out=kvg[:], in_=g.d["kvg"])], writes=[b_tab])
        load_w(c, wg, b_wg, wg_d, 8)
        load_w(c, wu, b_wu, wu_d, 8)
        if first:
            c.dma("pool", lambda e: e.dma_start(
                out=wlat[:], in_=g.d["w_in"][:, C_CKV:C_CKV + 288].rearrange("(k p) n -> p k n", p=128)),
                writes=[b_wlat])
        xT_stage(0)
        if nblk > 1:
            load_x(1)
        load_w(c, wd, b_wd, wd_d, NFC)
        for b in range(nblk):
            phase1(b)
            if first and b > 0:
                post(b - 1)
            phase2(b)
            if b + 1 < nblk:
                xT_stage(b + 1)
            if b + 2 < nblk:
                load_x(b + 2)
        if first:
            post(nblk - 1)
        c.barrier()


def layer_norm_tile(g, zt, b_z, yt, b_y, st, mv, rs, b_st, lng, lnb, b_ln):
    c = g.c
    c.op("dve", lambda e: e.bn_stats(out=st[:, 0, :], in_=zt[:, 0:512]), reads=[b_z], writes=[b_st])
    c.op("dve", lambda e: e.bn_stats(out=st[:, 1, :], in_=zt[:, 512:1024]), reads=[b_z], writes=[b_st])
    c.op("dve", lambda e: e.bn_aggr(out=mv[:], in_=st[:]), reads=[b_st], writes=[b_st])
    c.op("act", lambda e: e.activation(out=rs[:, 0:1], in_=mv[:, 1:2], func=AF.Sqrt, bias=g.epsc[:, 0:1], scale=1.0),
         reads=[b_st, g.b_const], writes=[b_st])
    c.op("dve", lambda e: e.reciprocal(out=rs[:, 0:1], in_=rs[:, 0:1]), reads=[b_st], writes=[b_st])
    c.op("dve", lambda e: e.scalar_tensor_tensor(out=rs[:, 1:2], in0=mv[:, 0:1], scalar=-1.0, in1=rs[:, 0:1],
                                                 op0=ALU.mult, op1=ALU.mult), reads=[b_st], writes=[b_st])
    c.op("act", lambda e: e.activation(out=yt[:], in_=zt[:], func=AF.Identity, bias=rs[:, 1:2], scale=rs[:, 0:1]),
         reads=[b_z, b_st], writes=[b_y])
    c.op("pool", lambda e: e.tensor_tensor(out=yt[:], in0=yt[:], in1=lng[:], op=ALU.mult),
         reads=[b_y, b_ln], writes=[b_y])
    c.op("pool", lambda e: e.tensor_tensor(out=yt[:], in0=yt[:], in1=lnb[:], op=ALU.add),
         reads=[b_y, b_ln], writes=[b_y])


def rms_tile(g, src, b_src, sq, b_sq, rs, b_st, gain, b_gain, dst, b_dst):
    c = g.c
    c.op("act", lambda e: e.activation(out=sq[:], in_=src, func=AF.Square), reads=[b_src], writes=[b_sq])
    c.op("dve", lambda e: e.reduce_sum(out=rs[:, 2:3], in_=sq[:], axis=AX.X), reads=[b_sq], writes=[b_st])
    c.op("act", lambda e: e.activation(out=rs[:, 3:4], in_=rs[:, 2:3], func=AF.Sqrt, bias=g.epsc[:, 0:1],
                                       scale=1.0 / 256), reads=[b_st, g.b_const], writes=[b_st])
    c.op("dve", lambda e: e.reciprocal(out=rs[:, 3:4], in_=rs[:, 3:4]), reads=[b_st], writes=[b_st])
    c.op("dve", lambda e: e.scalar_tensor_tensor(out=dst, in0=src, scalar=rs[:, 3:4], in1=gain[:],
                                                 op0=ALU.mult, op1=ALU.mult),
         reads=[b_src, b_st, b_gain], writes=[b_dst])


def rope_tile(g, x1, x2, b_x, cos, sin, b_tab, rt, b_rt, o1, o2, b_o):
    c = g.c
    sh = list(x1.shape)
    if len(sh) == 2:
        t = [rt[:, i, :] for i in range(4)]
    else:
        t = [rt[:, i] for i in range(4)]
    c.op("dve", lambda e: e.tensor_tensor(out=t[0], in0=x1, in1=cos, op=ALU.mult), reads=[b_x, b_tab], writes=[b_rt])
    c.op("dve", lambda e: e.tensor_tensor(out=t[1], in0=x2, in1=sin, op=ALU.mult), reads=[b_x, b_tab], writes=[b_rt])
    c.op("dve", lambda e: e.tensor_tensor(out=t[2], in0=x1, in1=sin, op=ALU.mult), reads=[b_x, b_tab], writes=[b_rt])
    c.op("dve", lambda e: e.tensor_tensor(out=t[3], in0=x2, in1=cos, op=ALU.mult), reads=[b_x, b_tab], writes=[b_rt])
    c.op("dve", lambda e: e.tensor_tensor(out=o1, in0=t[0], in1=t[1], op=ALU.subtract), reads=[b_rt], writes=[b_o])
    c.op("dve", lambda e: e.tensor_tensor(out=o2, in0=t[2], in1=t[3], op=ALU.add), reads=[b_rt], writes=[b_o])


def mla_phase(g):
    nc, c = g.nc, g.c
    with ExitStack() as es:
        def sb(name, shape, dt):
            return es.enter_context(nc.sbuf_tensor("m_" + name, shape, dt))

        def ps(name, shape, dt):
            return es.enter_context(nc.psum_tensor("m_" + name, shape, dt))

        h1o = sb("h1o", [128, 8, NOWN], BF16); b_h1o = Buf()
        knT = sb("knT", [128, 2, S], BF16); b_knT = Buf()
        kcat = sb("kcat", [96, S], BF16); b_kpe = Buf(); b_kn = [Buf() for _ in range(16)]
        qcT = sb("qcT", [96, 8, NOWN], BF16); b_qcT = Buf()
        qnT = sb("qnT", [128, 2, NOWN], BF16); b_qnT = Buf()
        wq = sb("wq", [128, 8, 256], BF16); wuq = sb("wuq", [128, 2, 768], BF16)
        wukv = sb("wukv", [128, 2, 1024], BF16); b_w = Buf()
        qg = sb("qg", [128, 256], F32); cosq = sb("cosq", [128, 16, 16], F32); sinq = sb("sinq", [128, 16, 16], F32)
        b_tab = Buf()
        vaug = sb("vaug", [128, 64, 65], BF16); b_vaug = Buf()
        sq = sb("sq", [128, 256], F32); b_sq = Buf()
        rs = sb("rs", [128, 4], F32); b_st = Buf()
        qn = sb("qn", [128, 256], BF16); b_qn = Buf()
        qf = sb("qf", [128, 8, 96], F32); b_qf = Buf()
        qc = sb("qc", [128, 8, 96], BF16); b_qc = Buf()
        rt = sb("rt", [128, 4, 8, 16], F32); b_rt = Buf()
        rrow = sb("rrow", [65, 512], F32); b_rrow = Buf()
        bc = sb("bc", [64, 512], F32); b_bc = Buf()
        ob = [sb("ob%d" % i, [64, 512], BF16) for i in range(2)]; b_ob = [Buf(), Buf()]
        pt2 = [sb("pt2%d" % i, [128, 2, 512], BF16) for i in range(2)]; b_pt2 = [Buf(), Buf()]
        qstack = ExitStack()

        def psq(name, shape, dt):
            return qstack.enter_context(nc.psum_tensor("m_" + name, shape, dt))

        pA = psq("qA", [128, 512], F32); b_pA = Buf()
        pS = [psq("qS%d" % i, [128, 512], F32) for i in range(2)]; b_pS = [Buf(), Buf()]
        pT = psq("qT", [128, 8, 128], BF16); b_pT = Buf()

        h1v = g.d["h1T"].rearrange("(k p) n -> p k n", p=128)
        for k in range(8):
            c.dma("sp", lambda e: e.dma_start(out=h1o[:, k, :], in_=h1v[:, k, 1024:3072]), writes=[b_h1o])
        c.dma("sp", [lambda e: e.dma_start(out=qg[:], in_=g.d["qg"]),
                     lambda e: e.dma_start(out=cosq[:], in_=g.d["cosk"][:, 0:16, :]),
                     lambda e: e.dma_start(out=sinq[:], in_=g.d["sink"][:, 0:16, :])], writes=[b_tab])
        c.dma("pool", [lambda e: e.dma_start(out=wq[:], in_=g.d["w_in"][:, C_CQ:C_CQ + 256].rearrange(
            "(k p) n -> p k n", p=128)),
            lambda e: e.dma_start(out=wuq[:], in_=g.d["w_uq"].rearrange("(k p) n -> p k n", p=128)),
            lambda e: e.dma_start(out=wukv[:], in_=g.d["w_ukv"].rearrange("(k p) n -> p k n", p=128))],
            writes=[b_w])
        knv = g.d["knT"].rearrange("(k p) n -> p k n", p=128)
        for k in range(2):
            c.dma("sp", lambda e: e.dma_start(out=knT[:, k, :], in_=knv[:, k, :]), writes=[b_knT])
        c.dma("sp", lambda e: e.dma_start(out=kcat[64:96, :], in_=g.d["kpeT"]), writes=[b_kpe])
        c.op("pool", lambda e: e.memset(vaug[:, :, 64:65], 1.0), writes=[b_vaug])

        for t in range(16):
            ts = slice(t * 128, (t + 1) * 128)
            mm(c, pA[:, 0:256], [(h1o[:, k, ts], wq[:, k, :]) for k in range(8)], [b_h1o, b_w], b_pA)
            rms_tile(g, pA[:, 0:256], b_pA, sq, b_sq, rs, b_st, qg, b_tab, qn[:], b_qn)
            c.chain("pe", [lambda e, k=k: e.transpose(out=pT[:, k, :], in_=qn[:, k * 128:(k + 1) * 128],
                                                      identity=g.ident[:]) for k in range(2)],
                    reads=[b_qn, g.b_const], writes=[b_pT])
            c.op("dve", lambda e: e.tensor_copy(out=qnT[:, :, ts], in_=pT[:, 0:2, :]), reads=[b_pT], writes=[b_qnT])
            mm(c, pS[0][:], [(qnT[:, k, ts], wuq[:, k, 0:512]) for k in range(2)], [b_qnT, b_w], b_pS[0])
            mm(c, pS[1][:, 0:256], [(qnT[:, k, ts], wuq[:, k, 512:768]) for k in range(2)], [b_qnT, b_w], b_pS[1])
            qf2 = qf[:].rearrange("p h d -> p (h d)")
            c.op("act", lambda e: e.activation(out=qf2[:, 0:512], in_=pS[0][:], func=AF.Copy),
                 reads=[b_pS[0]], writes=[b_qf])
            c.op("act", lambda e: e.activation(out=qf2[:, 512:768], in_=pS[1][:, 0:256], func=AF.Copy),
                 reads=[b_pS[1]], writes=[b_qf])
            c.op("act", lambda e: e.activation(out=qc[:, :, 0:64], in_=qf[:, :, 0:64], func=AF.Copy),
                 reads=[b_qf], writes=[b_qc])
            rope_tile(g, qf[:, :, 64:80], qf[:, :, 80:96], b_qf,
                      cosq[:, t:t + 1, :].broadcast_to([128, 8, 16]), sinq[:, t:t + 1, :].broadcast_to([128, 8, 16]),
                      b_tab, rt, b_rt, qc[:, :, 64:80], qc[:, :, 80:96], b_qc)
            c.chain("pe", [lambda e, h=h: e.transpose(out=pT[0:96, h, :], in_=qc[:, h, :], identity=g.ident[:])
                           for h in range(8)], reads=[b_qc, g.b_const], writes=[b_pT])
            c.op("dve", lambda e: e.tensor_copy(out=qcT[:, :, ts], in_=pT[0:96, :, :]), reads=[b_pT], writes=[b_qcT])

        c.barrier()
        qstack.close()
        pA = ps("pA", [128, 512], F32); b_pA = Buf()
        pB = ps("pB", [128, 512], F32); b_pB = Buf()
        pV = ps("pV", [128, 8, 64], F32); b_pV = Buf()
        pS2 = [ps("pS2%d" % i, [128, 2, 512], F32) for i in range(2)]; b_pS2 = [Buf(), Buf()]
        pO = ps("pO", [128, 512], F32); b_pO = Buf()
        for h in range(8):
            for kb in range(16):
                ks = slice(kb * 512, (kb + 1) * 512)
                pp, bp = (pA, b_pA) if kb % 2 == 0 else (pB, b_pB)
                mm(c, pp[0:64, :], [(wukv[:, k, h * 128:h * 128 + 64], knT[:, k, ks]) for k in range(2)],
                   [b_w, b_knT], bp)
                eng = "act" if kb % 2 == 0 else "dve"
                if eng == "act":
                    c.op("act", lambda e: e.activation(out=kcat[0:64, ks], in_=pp[0:64, :], func=AF.Copy),
                         reads=[bp], writes=[b_kn[kb]])
                else:
                    c.op("dve", lambda e: e.tensor_copy(out=kcat[0:64, ks], in_=pp[0:64, :]),
                         reads=[bp], writes=[b_kn[kb]])
            for kg in range(8):
                fns = []
                for j in range(8):
                    kt = kg * 8 + j
                    for k in range(2):
                        fns.append(lambda e, j=j, k=k, kt=kt: e.matmul(
                            pV[:, j, :], lhsT=knT[:, k, kt * 128:(kt + 1) * 128],
                            rhs=wukv[:, k, h * 128 + 64:h * 128 + 128], start=(k == 0), stop=(k == 1)))
                c.chain("pe", fns, reads=[b_w, b_knT], writes=[b_pV])
                c.op("dve", lambda e: e.tensor_copy(out=vaug[:, kg * 8:(kg + 1) * 8, 0:64], in_=pV[:]),
                     reads=[b_pV], writes=[b_vaug])
            for qb in range(4):
                qs = slice(qb * 512, (qb + 1) * 512)
                def s_pair(n):
                    si = n % 2
                    c.chain("pe", [lambda e, j=j: e.matmul(pS2[si][:, j, :],
                                                           lhsT=kcat[0:96, (2 * n + j) * 128:(2 * n + j + 1) * 128],
                                                           rhs=qcT[0:96, h, qs], start=True, stop=True)
                                   for j in range(2)],
                            reads=[b_kn[(2 * n) // 4], b_kpe, b_qcT], writes=[b_pS2[si]])

                s_pair(0)
                for n in range(32):
                    si = n % 2
                    if n + 1 < 32:
                        s_pair(n + 1)
                    c.op("act", lambda e: e.activation(out=pt2[si][:], in_=pS2[si][:], func=AF.Exp, scale=MLA_SCALE),
                         reads=[b_pS2[si]], writes=[b_pt2[si]])
                    c.chain("pe", [lambda e, j=j: e.matmul(pO[0:65, :], lhsT=vaug[:, 2 * n + j, 0:65], rhs=pt2[si][:, j, :],
                                                           start=(n == 0 and j == 0), stop=(n == 31 and j == 1))
                                   for j in range(2)],
                            reads=[b_pt2[si], b_vaug], writes=[b_pO])
                oi = qb % 2
                c.op("dve", lambda e: e.reciprocal(out=rrow[64:65, :], in_=pO[64:65, :]), reads=[b_pO], writes=[b_rrow])
                c.op("pe", lambda e: e.matmul(pB[0:64, :], lhsT=g.onesf[64:65, 0:64], rhs=rrow[64:65, :],
                                              start=True, stop=True), reads=[b_rrow, g.b_const], writes=[b_pB])
                c.op("act", lambda e: e.activation(out=bc[:], in_=pB[0:64, :], func=AF.Copy), reads=[b_pB], writes=[b_bc])
                c.op("dve", lambda e: e.tensor_tensor(out=ob[oi][:], in0=pO[0:64, :], in1=bc[:], op=ALU.mult),
                     reads=[b_pO, b_bc], writes=[b_ob[oi]])
                c.dma("sp", lambda e: e.dma_start(out=g.d["omla"][h * 64:(h + 1) * 64, qs], in_=ob[oi][:]),
                      reads=[b_ob[oi]], writes=[g.b_omla])
        c.barrier()


def dil_phase(g):
    nc, c = g.nc, g.c
    with ExitStack() as es:
        def sb(name, shape, dt):
            return es.enter_context(nc.sbuf_tensor("d_" + name, shape, dt))

        def ps(name, shape, dt):
            return es.enter_context(nc.psum_tensor("d_" + name, shape, dt))

        h1e = sb("h1e", [128, 8, 4096], BF16); b_h1e = Buf()
        etab = sb("etab", [128, 12, 2, 128], F32); vld = sb("vld", [128, 69], F32); b_tab = Buf()
        numT = sb("numT", [128, NOWN], F32); denT = sb("denT", [1, NOWN], F32); b_acc = Buf()
        qd = sb("qd", [128, NOWN], BF16); b_qd = Buf()
        kd = sb("kd", [128, 4096], BF16); b_kd = Buf()
        va = sb("va", [128, 32, 129], BF16); b_va = Buf()
        wqkv = [sb("wqkv%d" % i, [128, 8, 3, 128], BF16) for i in range(2)]; b_wqkv = [Buf(), Buf()]
        ex = [sb("ex%d" % i, [128, 128], F32) for i in range(2)]; b_ex = [Buf(), Buf()]
        p2 = [sb("p2%d" % i, [128, 128], BF16) for i in range(2)]; b_p2 = [Buf(), Buf()]
        bc = sb("bc", [128, 512], F32); b_bc = Buf()
        ob = [sb("ob%d" % i, [128, 512], BF16) for i in range(2)]; b_ob = [Buf(), Buf()]
        rrow = sb("rrow", [1, 512], F32); b_rrow = Buf()
        memb = sb("memb", [128, 2, D], BF16); b_memb = Buf()
        memT = sb("memT", [128, 8, 256], BF16); b_memT = Buf()
        wmkv = sb("wmkv", [128, 8, D], BF16); wmq = sb("wmq", [128, 8, 512], BF16); b_wm = Buf()
        kmT = sb("kmT", [128, 4, 256], BF16); vm = sb("vm", [128, 2, 512], BF16); b_km = Buf()
        qm = sb("qm", [128, 512], BF16); b_qm = Buf()
        ptm = [sb("ptm%d" % i, [128, 512], BF16) for i in range(2)]; b_ptm = [Buf(), Buf()]
        pA = ps("pA", [128, 512], F32); b_pA = Buf()
        pB = ps("pB", [128, 512], F32); b_pB = Buf()
        pV = [ps("pV%d" % i, [128, 512], F32) for i in range(2)]; b_pV = [Buf(), Buf()]
        pS = [ps("pS%d" % i, [128, 512], F32) for i in range(2)]; b_pS = [Buf(), Buf()]
        pN = ps("pN", [128, 512], F32); b_pN = Buf()
        pT = ps("pT", [128, 8, 128], BF16); b_pT = Buf()

        h1v = g.d["h1T"].rearrange("(k p) n -> p k n", p=128)
        for k in range(8):
            c.dma("sp", lambda e: e.dma_start(out=h1e[:, k, :], in_=h1v[:, k, :]), writes=[b_h1e])
        c.dma("sp", [lambda e: e.dma_start(out=etab[:].rearrange("p a b c -> p (a b c)"), in_=g.d["etab"]),
                     lambda e: e.dma_start(out=vld[:], in_=g.d["vld"])], writes=[b_tab])

        def load_wqkv(idx):
            hh, gi = idx // 3, idx % 3
            i = idx % 2
            fns = []
            for qkv in range(3):
                col = C_DIL + ((qkv * 3 + gi) * 4 + hh) * 128
                fns.append(lambda e, qkv=qkv, col=col: e.dma_start(
                    out=wqkv[i][:, :, qkv, :], in_=g.d["w_in"][:, col:col + 128].rearrange("(k p) n -> p k n", p=128)))
            c.dma("pool", fns, writes=[b_wqkv[i]])

        load_wqkv(0)
        c.dma("pool", [lambda e: e.dma_start(out=memb[:], in_=g.d["mem"].rearrange("(t p) n -> p t n", p=128)),
                       lambda e: e.dma_start(out=wmq[:], in_=g.d["w_in"][:, C_MEMQ:C_MEMQ + 512].rearrange(
                           "(k p) n -> p k n", p=128))], writes=[b_memb, b_wm])
        load_w(c, wmkv, b_wm, g.d["w_mem_kv"], 8)

        base_t = [0, 17, 37]
        cnt = 0
        for hh in range(4):
            for gi, (win, d) in enumerate(DIL):
                idx = hh * 3 + gi
                wi = idx % 2
                if idx + 1 < 12:
                    load_wqkv(idx + 1)
                w3 = wqkv[wi]
                hd = gi * 4 + hh
                nt = (NOWN // d) // 128 + 1
                for qb in range(4):
                    cs = slice(1024 + qb * 512, 1024 + (qb + 1) * 512)
                    mm(c, pA[:], [(w3[:, k, 0, :], h1e[:, k, cs]) for k in range(8)], [b_wqkv[wi], b_h1e], b_pA)
                    c.op("act", lambda e: e.activation(out=qd[:, qb * 512:(qb + 1) * 512], in_=pA[:], func=AF.Copy),
                         reads=[b_pA], writes=[b_qd])
                lo = (1024 - 64 * d) // 512
                hi = -(-(3072 + 64 * d) // 512)
                for eb in range(lo, hi):
                    cs = slice(eb * 512, (eb + 1) * 512)
                    mm(c, pB[:], [(w3[:, k, 1, :], h1e[:, k, cs]) for k in range(8)], [b_wqkv[wi], b_h1e], b_pB)
                    c.op("dve", lambda e: e.tensor_copy(out=kd[:, cs], in_=pB[:]), reads=[b_pB], writes=[b_kd])
                for cl in range(d):
                    for m in range(nt):
                        ti = cl * nt + m
                        st0 = 1024 + cl + d * (128 * m - 64)
                        ksl = slice(st0, st0 + 127 * d + 1, d)
                        vi = cnt % 2
                        cnt += 1
                        mm(c, pV[vi][:, 0:128], [(h1e[:, k, ksl], w3[:, k, 2, :]) for k in range(8)],
                           [b_wqkv[wi], b_h1e], b_pV[vi])
                        col = base_t[gi] + ti
                        c.op("dve", lambda e: e.tensor_scalar(out=va[:, ti, 0:128], in0=pV[vi][:, 0:128],
                                                              scalar1=vld[:, col:col + 1], scalar2=None, op0=ALU.mult),
                             reads=[b_pV[vi], b_tab], writes=[b_va])
                        c.op("act", lambda e: e.activation(out=va[:, ti, 128:129], in_=vld[:, col:col + 1], func=AF.Copy),
                             reads=[b_tab], writes=[b_va])
                nq = (NOWN // d) // 128
                for cl in range(d):
                    for mq in range(nq):
                        q0 = cl + d * 128 * mq
                        qsl = slice(q0, q0 + 127 * d + 1, d)
                        for side in range(2):
                            st0 = 1024 + cl + d * (128 * (mq + side) - 64)
                            ksl = slice(st0, st0 + 127 * d + 1, d)
                            c.op("pe", lambda e: e.matmul(pS[side][:, 0:128], lhsT=kd[:, ksl], rhs=qd[:, qsl],
                                                          start=True, stop=True),
                                 reads=[b_kd, b_qd], writes=[b_pS[side]])
                            c.op("act", lambda e: e.activation(out=ex[side][:], in_=pS[side][:, 0:128], func=AF.Exp,
                                                               scale=DH_SCALE), reads=[b_pS[side]], writes=[b_ex[side]])
                            c.op("dve", lambda e: e.tensor_tensor(out=p2[side][:], in0=ex[side][:],
                                                                  in1=etab[:, hd, side, :], op=ALU.mult),
                                 reads=[b_ex[side], b_tab], writes=[b_p2[side]])
                        tis = [cl * nt + mq, cl * nt + mq + 1]
                        c.chain("pe", [lambda e, s_=s_: e.matmul(pN[:, 0:128], lhsT=va[:, tis[s_], 0:128], rhs=p2[s_][:],
                                                                 start=(s_ == 0), stop=(s_ == 1)) for s_ in range(2)] +
                                [lambda e, s_=s_: e.matmul(pN[0:1, 128:256], lhsT=va[:, tis[s_], 128:129], rhs=p2[s_][:],
                                                           start=(s_ == 0), stop=(s_ == 1)) for s_ in range(2)],
                                reads=[b_va, b_p2[0], b_p2[1]], writes=[b_pN])
                        if gi == 0:
                            c.op("act", lambda e: e.activation(out=numT[:, qsl], in_=pN[:, 0:128], func=AF.Copy),
                                 reads=[b_pN], writes=[b_acc])
                            c.op("act", lambda e: e.activation(out=denT[0:1, qsl], in_=pN[0:1, 128:256], func=AF.Copy),
                                 reads=[b_pN], writes=[b_acc])
                        else:
                            c.op("dve", lambda e: e.tensor_tensor(out=numT[:, qsl], in0=pN[:, 0:128], in1=numT[:, qsl],
                                                                  op=ALU.add), reads=[b_pN, b_acc], writes=[b_acc])
                            c.op("dve", lambda e: e.tensor_tensor(out=denT[0:1, qsl], in0=pN[0:1, 128:256],
                                                                  in1=denT[0:1, qsl], op=ALU.add),
                                 reads=[b_pN, b_acc], writes=[b_acc])
            for qb in range(4):
                qs = slice(qb * 512, (qb + 1) * 512)
                oi = qb % 2
                c.op("dve", lambda e: e.reciprocal(out=rrow[:], in_=denT[0:1, qs]), reads=[b_acc], writes=[b_rrow])
                c.op("pe", lambda e: e.matmul(pA[:], lhsT=g.onesf[0:1, :], rhs=rrow[:], start=True, stop=True),
                     reads=[b_rrow, g.b_const], writes=[b_pA])
                c.op("act", lambda e: e.activation(out=bc[:], in_=pA[:], func=AF.Copy), reads=[b_pA], writes=[b_bc])
                c.op("dve", lambda e: e.tensor_tensor(out=ob[oi][:], in0=numT[:, qs], in1=bc[:], op=ALU.mult),
                     reads=[b_acc, b_bc], writes=[b_ob[oi]])
                c.dma("sp", lambda e: e.dma_start(out=g.d["odil"][hh * 128:(hh + 1) * 128, qs], in_=ob[oi][:]),
                      reads=[b_ob[oi]], writes=[g.b_odil])

        for t in range(2):
            c.chain("pe", [lambda e, k=k: e.transpose(out=pT[:, k, :], in_=memb[:, t, k * 128:(k + 1) * 128],
                                                      identity=g.ident[:]) for k in range(8)],
                    reads=[b_memb, g.b_const], writes=[b_pT])
            c.op("dve", lambda e: e.tensor_copy(out=memT[:, :, t * 128:(t + 1) * 128], in_=pT[:]),
                 reads=[b_pT], writes=[b_memT])
        for h in range(4):
            mm(c, pA[:, 0:256], [(wmkv[:, k, h * 128:(h + 1) * 128], memT[:, k, :]) for k in range(8)],
               [b_wm, b_memT], b_pA)
            c.op("act", lambda e: e.activation(out=kmT[:, h, :], in_=pA[:, 0:256], func=AF.Copy),
                 reads=[b_pA], writes=[b_km])
        for kt in range(2):
            mm(c, pB[:], [(memT[:, k, kt * 128:(kt + 1) * 128], wmkv[:, k, 512:1024]) for k in range(8)],
               [b_wm, b_memT], b_pB)
            c.op("dve", lambda e: e.tensor_copy(out=vm[:, kt, :], in_=pB[:]), reads=[b_pB], writes=[b_km])
        for h in range(4):
            for qb in range(4):
                qs = slice(qb * 512, (qb + 1) * 512)
                cs = slice(1024 + qb * 512, 1024 + (qb + 1) * 512)
                oi = qb % 2
                mm(c, pB[:], [(wmq[:, k, h * 128:(h + 1) * 128], h1e[:, k, cs]) for k in range(8)], [b_wm, b_h1e], b_pB)
                c.op("dve", lambda e: e.tensor_copy(out=qm[:], in_=pB[:]), reads=[b_pB], writes=[b_qm])
                for kt in range(2):
                    c.op("pe", lambda e: e.matmul(pS[kt][:], lhsT=kmT[:, h, kt * 128:(kt + 1) * 128], rhs=qm[:],
                                                  start=True, stop=True), reads=[b_km, b_qm], writes=[b_pS[kt]])
                    c.op("act", lambda e: e.activation(out=ptm[kt][:], in_=pS[kt][:], func=AF.Exp, scale=DH_SCALE),
                         reads=[b_pS[kt]], writes=[b_ptm[kt]])
                c.chain("pe", [lambda e, kt=kt: e.matmul(pV[0][:], lhsT=vm[:, kt, h * 128:(h + 1) * 128], rhs=ptm[kt][:],
                                                         start=(kt == 0), stop=(kt == 1)) for kt in range(2)] +
                        [lambda e, kt=kt: e.matmul(pV[1][0:1, :], lhsT=g.onesb[:, 0:1], rhs=ptm[kt][:],
                                                   start=(kt == 0), stop=(kt == 1)) for kt in range(2)],
                        reads=[b_km, b_ptm[0], b_ptm[1], g.b_const], writes=[b_pV[0], b_pV[1]])
                c.op("dve", lambda e: e.reciprocal(out=rrow[:], in_=pV[1][0:1, :]), reads=[b_pV[1]], writes=[b_rrow])
                c.op("pe", lambda e: e.matmul(pA[:], lhsT=g.onesf[0:1, :], rhs=rrow[:], start=True, stop=True),
                     reads=[b_rrow, g.b_const], writes=[b_pA])
                c.op("act", lambda e: e.activation(out=bc[:], in_=pA[:], func=AF.Copy), reads=[b_pA], writes=[b_bc])
                c.op("dve", lambda e: e.tensor_tensor(out=ob[oi][:], in0=pV[0][:], in1=bc[:], op=ALU.mult),
                     reads=[b_pV[0], b_bc], writes=[b_ob[oi]])
                c.dma("sp", lambda e: e.dma_start(out=g.d["omem"][h * 128:(h + 1) * 128, qs], in_=ob[oi][:]),
                      reads=[b_ob[oi]], writes=[g.b_omem])
        c.barrier()


def merge_phase(g):
    nc, c = g.nc, g.c
    with ExitStack() as es:
        def sb(name, shape, dt):
            return es.enter_context(nc.sbuf_tensor("g_" + name, shape, dt))

        def ps(name, shape, dt):
            return es.enter_context(nc.psum_tensor("g_" + name, shape, dt))

        wgate = sb("wgate", [128, 8, 3072], BF16); b_wgate = Buf()
        wbm = sb("wbm", [64, 8, D], BF16); wbd = sb("wbd", [128, 4, D], BF16); wbe = sb("wbe", [128, 4, D], BF16)
        wo = sb("wo", [128, 8, D], BF16); b_w = Buf()
        lng = sb("lng", [128, D], F32); lnb = sb("lnb", [128, D], F32); b_ln = Buf()
        h1b = [sb("h1b%d" % i, [128, 8, 512], BF16) for i in range(2)]
        omb = [sb("omb%d" % i, [64, 8, 512], BF16) for i in range(2)]
        odb = [sb("odb%d" % i, [128, 4, 512], BF16) for i in range(2)]
        oeb = [sb("oeb%d" % i, [128, 4, 512], BF16) for i in range(2)]
        b_in = [Buf(), Buf()]
        sg = [sb("sg%d" % i, [128, 512], F32) for i in range(2)]; b_sg = [Buf(), Buf()]
        tmp = [sb("tmp%d" % i, [128, 512], F32) for i in range(2)]; b_tmp = [Buf(), Buf()]
        acc = [sb("acc%d" % i, [128, 512], F32) for i in range(2)]; b_accs = [Buf(), Buf()]
        mT = sb("mT", [128, 8, 512], BF16); b_mT = Buf()
        xr = [sb("xr%d" % i, [128, D], F32) for i in range(2)]; b_xr = [Buf(), Buf()]
        z = sb("z", [128, D], F32); b_z = Buf()
        yn = [sb("yn%d" % i, [128, D], F32) for i in range(2)]; b_yn = [Buf(), Buf()]
        st = sb("st", [128, 2, 6], F32); mv = sb("mv", [128, 2], F32); rs = sb("rs", [128, 4], F32); b_st = Buf()
        pY = [ps("pY%d" % i, [128, 512], F32) for i in range(2)]; b_pY = [Buf(), Buf()]
        pG = [ps("pG%d" % i, [128, 512], F32) for i in range(2)]; b_pG = [Buf(), Buf()]
        pZ = [ps("pZ%d" % i, [128, 512], F32) for i in range(2)]; b_pZ = [Buf(), Buf()]

        h1v = g.d["h1T"].rearrange("(k p) n -> p k n", p=128)
        omv = g.d["omla"].rearrange("(h p) n -> p h n", p=64)
        odv = g.d["odil"].rearrange("(h p) n -> p h n", p=128)
        oev = g.d["omem"].rearrange("(h p) n -> p h n", p=128)

        def load_blk(qb):
            i = qb % 2
            qs = slice(qb * 512, (qb + 1) * 512)
            c.dma("sp", [lambda e: e.dma_start(out=h1b[i][:], in_=h1v[:, :, 1024 + qb * 512:1024 + (qb + 1) * 512]),
                         lambda e: e.dma_start(out=omb[i][:], in_=omv[:, :, qs]),
                         lambda e: e.dma_start(out=odb[i][:], in_=odv[:, :, qs]),
                         lambda e: e.dma_start(out=oeb[i][:], in_=oev[:, :, qs])], writes=[b_in[i]])

        def load_xr(gt):
            i = gt % 2
            c.dma("sp", lambda e: e.dma_start(out=xr[i][:], in_=g.d["h1own"][gt * 128:(gt + 1) * 128, :]),
                  writes=[b_xr[i]])

        load_blk(0)
        c.dma("sp", [lambda e: e.dma_start(out=lng[:], in_=g.d["lnp"][2]),
                     lambda e: e.dma_start(out=lnb[:], in_=g.d["lnp"][3])], writes=[b_ln])
        c.dma("pool", [lambda e: e.dma_start(out=wbm[:], in_=g.d["w_br_mla"].rearrange("(h p) n -> p h n", p=64)),
                       lambda e: e.dma_start(out=wbd[:], in_=g.d["w_br_dil"].rearrange("(h p) n -> p h n", p=128)),
                       lambda e: e.dma_start(out=wbe[:], in_=g.d["w_br_mem"].rearrange("(h p) n -> p h n", p=128))],
              writes=[b_w])
        for k in range(8):
            c.dma("pool", lambda e: e.dma_start(out=wgate[:, k, :], in_=g.d["w_in"][k * 128:(k + 1) * 128, C_GATE:C_GATE + 3072]),
                  writes=[b_wgate])
        load_w(c, wo, b_w, g.d["w_o"], 8)
        load_xr(0)
        n = 0
        for qb in range(4):
            i = qb % 2
            if qb + 1 < 4:
                load_blk(qb + 1)
            for fc in range(8):
                fs = slice(fc * 128, (fc + 1) * 128)
                ai = fc % 2
                for br in range(3):
                    j = n % 2
                    n += 1
                    if br == 0:
                        pairs = [(wbm[0:64, h, fs], omb[i][0:64, h, :]) for h in range(8)]
                    elif br == 1:
                        pairs = [(wbd[:, h, fs], odb[i][:, h, :]) for h in range(4)]
                    else:
                        pairs = [(wbe[:, h, fs], oeb[i][:, h, :]) for h in range(4)]
                    mm(c, pY[j][:], pairs, [b_w, b_in[i]], b_pY[j])
                    gc = br * 1024 + fc * 128
                    mm(c, pG[j][:], [(wgate[:, k, gc:gc + 128], h1b[i][:, k, :]) for k in range(8)],
                       [b_wgate, b_in[i]], b_pG[j])
                    c.op("act", lambda e: e.activation(out=sg[j][:], in_=pG[j][:], func=AF.Sigmoid),
                         reads=[b_pG[j]], writes=[b_sg[j]])
                    if br == 0:
                        c.op("dve", lambda e: e.tensor_tensor(out=acc[ai][:], in0=sg[j][:], in1=pY[j][:], op=ALU.mult),
                             reads=[b_sg[j], b_pY[j]], writes=[b_accs[ai]])
                    else:
                        c.op("dve", lambda e: e.tensor_tensor(out=tmp[j][:], in0=sg[j][:], in1=pY[j][:], op=ALU.mult),
                             reads=[b_sg[j], b_pY[j]], writes=[b_tmp[j]])
                        if br == 1:
                            c.op("pool", lambda e: e.tensor_tensor(out=acc[ai][:], in0=acc[ai][:], in1=tmp[j][:],
                                                                   op=ALU.add),
                                 reads=[b_accs[ai], b_tmp[j]], writes=[b_accs[ai]])
                        else:
                            c.op("pool", lambda e: e.tensor_tensor(out=mT[:, fc, :], in0=acc[ai][:], in1=tmp[j][:],
                                                                   op=ALU.add),
                                 reads=[b_accs[ai], b_tmp[j]], writes=[b_mT])
            for t in range(4):
                gt = qb * 4 + t
                ri = gt % 2
                if gt + 1 < 16:
                    load_xr(gt + 1)
                ts = slice(t * 128, (t + 1) * 128)
                for h in range(2):
                    mm(c, pZ[h][:], [(mT[:, fc, ts], wo[:, fc, h * 512:(h + 1) * 512]) for fc in range(8)],
                       [b_mT, b_w], b_pZ[h])
                c.op("act", lambda e: e.activation(out=z[:], in_=xr[ri][:], func=AF.Copy, scale=ALPHA),
                     reads=[b_xr[ri]], writes=[b_z])
                for h in range(2):
                    hs = slice(h * 512, (h + 1) * 512)
                    c.op("dve", lambda e: e.tensor_tensor(out=z[:, hs], in0=pZ[h][:], in1=z[:, hs], op=ALU.add),
                         reads=[b_pZ[h], b_z], writes=[b_z])
                layer_norm_tile(g, z, b_z, yn[ri], b_yn[ri], st, mv, rs, b_st, lng, lnb, b_ln)
                c.dma("sp", lambda e: e.dma_start(out=g.d["h2"][gt * 128:(gt + 1) * 128, :], in_=yn[ri][:]),
                      reads=[b_yn[ri]], writes=[g.b_h2])
        c.barrier()


def build(stage=99, debug=False):
    nc = bass.Bass("TRN2", target_bir_lowering=False)
    g = K()
    g.nc = nc
    g.c = Ctx(nc)
    g.d = {}

    def din(name, shape, dt=F32):
        g.d[name] = nc.dram_tensor(name, shape, dt, kind="ExternalInput").ap()

    def dscr(name, shape, dt):
        g.d[name] = nc.dram_tensor(name, shape, dt, kind="ExternalOutput" if debug else "Internal").ap()

    din("xctx", [S, D]); din("mem", [256, D]); din("w_in", [D, 8736])
    din("qg", [128, 256]); din("kvg", [128, 256]); din("w_uq", [256, 768]); din("w_ukv", [256, 1024])
    din("w_mem_kv", [D, D]); din("w_br_mla", [512, D]); din("w_br_dil", [512, D]); din("w_br_mem", [512, D])
    din("w_o", [D, D])
    for f in ("f1", "f2"):
        din(f + "g", [D, FF]); din(f + "u", [D, FF]); din(f + "d", [FF, D])
    din("lnp", [6, 128, D]); din("cosk", [128, 64, 16]); din("sink", [128, 64, 16])
    din("vld", [128, 69]); din("etab", [128, 12 * 2 * 128]); din("identf", [128, 128])
    g.d["out"] = nc.dram_tensor("out", [NOWN, D], F32, kind="ExternalOutput").ap()
    dscr("h1own", [NOWN, D], F32); dscr("h1T", [D, 4096], BF16)
    dscr("knT", [256, S], BF16); dscr("kpeT", [32, S], BF16)
    dscr("omla", [512, NOWN], BF16); dscr("odil", [512, NOWN], BF16); dscr("omem", [512, NOWN], BF16)
    dscr("h2", [NOWN, D], F32)
    for n in ("h1own", "h1T", "knT", "omla", "odil", "omem", "h2", "out"):
        setattr(g, "b_" + n, Buf(n))
    c = g.c
    with ExitStack() as es:
        g.ident = es.enter_context(nc.sbuf_tensor("ident", [128, 128], BF16))
        g.onesb = es.enter_context(nc.sbuf_tensor("onesb", [128, 128], BF16))
        g.onesf = es.enter_context(nc.sbuf_tensor("onesf", [128, 128], F32))
        idf = es.enter_context(nc.sbuf_tensor("idf", [128, 128], F32))
        g.b_const = Buf("const")
        c.dma("sp", lambda e: e.dma_start(out=idf[:], in_=g.d["identf"]), writes=[g.b_const])
        c.op("dve", lambda e: e.tensor_copy(out=g.ident[:], in_=idf[:]), reads=[g.b_const], writes=[g.b_const])
        c.op("pool", lambda e: e.memset(g.onesb[:], 1.0), writes=[g.b_const])
        c.op("pool", lambda e: e.memset(g.onesf[:], 1.0), writes=[g.b_const])
        g.epsc = es.enter_context(nc.sbuf_tensor("epsc", [128, 1], F32))
        c.op("pool", lambda e: e.memset(g.epsc[:], EPS), writes=[g.b_const])
        ffn_phase(g, g.d["xctx"], 32 if stage != 0 else 2, g.d["f1g"], g.d["f1u"], g.d["f1d"],
                  g.d["lnp"][0], g.d["lnp"][1], True, "a_")
        if stage >= 2:
            mla_phase(g)
        if stage >= 3:
            dil_phase(g)
        if stage >= 4:
            merge_phase(g)
        if stage >= 5:
            ffn_phase(g, g.d["h2"], 8, g.d["f2g"], g.d["f2u"], g.d["f2d"], g.d["lnp"][4], g.d["lnp"][5],
                      False, "f_")
        c.finish("sp")
    return nc


def host_tables(r):
    order = [r, (r + 3) % 4, (r + 1) % 4, (r + 2) % 4]
    pos = np.concatenate([np.arange(o * NOWN, (o + 1) * NOWN) for o in order]).astype(np.float32)
    inv = 1.0 / (10000.0 ** (np.arange(0, 32, 2, dtype=np.float32) / 32))
    ang = pos[:, None] * inv[None, :]
    cosk = np.cos(ang).astype(np.float32).reshape(64, 128, 16).transpose(1, 0, 2)
    sink = np.sin(ang).astype(np.float32).reshape(64, 128, 16).transpose(1, 0, 2)
    q0 = r * NOWN
    epos = q0 - 1024 + np.arange(4096)
    evalid = ((epos >= 0) & (epos < S)).astype(np.float32)
    cols = []
    for (win, d) in DIL:
        nt = (NOWN // d) // 128 + 1
        for cl in range(d):
            for m in range(nt):
                sub = -64 + 128 * m + np.arange(128)
                e = 1024 + cl + d * sub
                cols.append(evalid[e])
    vld = np.stack(cols, 1).astype(np.float32)
    return order, np.ascontiguousarray(cosk), np.ascontiguousarray(sink), vld


def etab_table():
    slopes = 2.0 ** (-8.0 * np.arange(1, 13, dtype=np.float32) / 12)
    i = np.arange(128)[:, None]
    j = np.arange(128)[None, :]
    tab = np.zeros((128, 12, 2, 128), np.float32)
    for gi, (win, d) in enumerate(DIL):
        for hh in range(4):
            hd = gi * 4 + hh
            for side, off in enumerate((-64, 64)):
                rel = i + off - j
                tab[:, hd, side, :] = np.where(np.abs(rel) <= 64, np.exp(-slopes[hd] * d * np.abs(rel)), 0.0)
    return np.ascontiguousarray(tab.reshape(128, -1))


def make_in_maps(inp):
    f = lambda a: np.ascontiguousarray(np.asarray(a, dtype=np.float32))
    x = f(inp["x"]); mem = f(inp["mem"])
    rep = lambda v, n: np.ascontiguousarray(np.broadcast_to(f(v).reshape(1, n), (128, n)))
    common = {
        "w_in": f(inp["w_in"])[0], "qg": rep(inp["mla_q_norm"], 256), "kvg": rep(inp["mla_kv_norm"], 256),
        "w_uq": f(inp["w_uq"])[0], "w_ukv": f(inp["w_ukv"])[0], "w_mem_kv": f(inp["w_mem_kv"])[0],
        "w_br_mla": f(inp["w_br_mla"])[0], "w_br_dil": f(inp["w_br_dil"])[0], "w_br_mem": f(inp["w_br_mem"])[0],
        "w_o": f(inp["w_o"])[0],
        "f1g": f(inp["ffn1_w_gate"])[0], "f1u": f(inp["ffn1_w_up"])[0], "f1d": f(inp["ffn1_w_down"])[0],
        "f2g": f(inp["ffn2_w_gate"])[0], "f2u": f(inp["ffn2_w_up"])[0], "f2d": f(inp["ffn2_w_down"])[0],
        "lnp": np.ascontiguousarray(np.stack([rep(inp[k], D) for k in
                                              ("ln1_g", "ln1_b", "ln2_g", "ln2_b", "ln3_g", "ln3_b")], 0)),
        "etab": etab_table(), "identf": np.eye(128, dtype=np.float32),
    }
    maps = []
    for core in range(8):
        b, r = core // 4, core % 4
        order, cosk, sink, vld = host_tables(r)
        xctx = np.ascontiguousarray(np.concatenate([x[b, o * NOWN:(o + 1) * NOWN] for o in order], 0))
        m = dict(common)
        m.update({"xctx": xctx, "mem": np.ascontiguousarray(mem[b]), "cosk": cosk, "sink": sink, "vld": vld})
        maps.append(m)
    return maps


_NC_CACHE = {}


def kernel(**inputs):
    if "nc" not in _NC_CACHE:
        _NC_CACHE["nc"] = build()
    nc = _NC_CACHE["nc"]
    maps = make_in_maps(inputs)
    res = run_bass_kernel_spmd(nc, maps, core_ids=list(range(8)))
    out = np.zeros((2, S, D), np.float32)
    for core in range(8):
        b, r = core // 4, core % 4
        out[b, r * NOWN:(r + 1) * NOWN] = res.results[core]["out"]
    return out
```
